# Optimizing a Trainium2 kernel written in Bass

```python
import jax, jax.numpy as jnp
from jax import lax
import numpy as np

D_MODEL = 2048
BATCH = 2
SEQ = 4096
DEPTH = 1

POOL_WIDTH = D_MODEL // 2
POOL_WINDOWS = (2, 4, 8, 16)
N_POOL_GROUPS = len(POOL_WINDOWS)
POOL_GROUP_DIM = POOL_WIDTH // N_POOL_GROUPS
HGRN_WIDTH = D_MODEL - POOL_WIDTH
HGRN_DK = 128
HGRN_HEADS = HGRN_WIDTH // HGRN_DK
HGRN_DV = HGRN_WIDTH // HGRN_HEADS
HGRN_KEY_WIDTH = HGRN_HEADS * HGRN_DK
CHUNK = 64
IN_COLS = POOL_WIDTH + 2 * HGRN_KEY_WIDTH + 2 * HGRN_WIDTH
D_FF = 5632
CONV_WIDTH = 3
ALPHA = (2.0 * DEPTH) ** 0.25
BETA = (8.0 * DEPTH) ** -0.25
LN_EPS = 1e-5
RMS_EPS = 1e-6

kernel_name = "hymba_pool_hgrn2_convffn_deepnorm"


def layer_norm(x, g, b):
    x32 = x.astype(jnp.float32)
    mu = jnp.mean(x32, axis=-1, keepdims=True)
    var = jnp.mean(jnp.square(x32 - mu), axis=-1, keepdims=True)
    y = (x32 - mu) * lax.rsqrt(var + LN_EPS) * g.astype(jnp.float32) + b.astype(jnp.float32)
    return y.astype(x.dtype)


def pool_mixer(u, pool_w, pool_b, pool_scale):
    B, S, _ = u.shape
    u32 = u.astype(jnp.float32).reshape(B, S, N_POOL_GROUPS, POOL_GROUP_DIM)
    cs = jnp.cumsum(u32, axis=1)
    pos = jnp.arange(S)
    pooled = []
    for gi, w in enumerate(POOL_WINDOWS):
        c = cs[:, :, gi]
        c_prev = jnp.pad(c, ((0, 0), (w, 0), (0, 0)))[:, :S]
        cnt = jnp.minimum(pos + 1, w).astype(jnp.float32)[None, :, None]
        pooled.append((c - c_prev) / cnt)
    pooled = jnp.stack(pooled, axis=2) - u32
    y = jnp.einsum('bsgc,gcd->bsgd', pooled.astype(u.dtype), pool_w) + pool_b
    return y.reshape(B, S, POOL_WIDTH) * pool_scale


def hgrn2_mixer(q, f_logit, v, gate, lb, g_norm):
    B, S, _ = q.shape
    H, dk, dv, C = HGRN_HEADS, HGRN_DK, HGRN_DV, CHUNK
    n_chunks = S // C
    lb32 = lb.astype(jnp.float32)
    f = lb32 + (1.0 - lb32) * jax.nn.sigmoid(f_logit.astype(jnp.float32))
    log_f = jnp.log(f)
    k = 1.0 - f

    def to_chunks(t, d):
        return t.astype(jnp.float32).reshape(B, n_chunks, C, H, d).transpose(1, 0, 3, 2, 4)

    qc, kc, gc, vc = to_chunks(q, dk), to_chunks(k, dk), to_chunks(log_f, dk), to_chunks(v, dv)
    causal = jnp.tril(jnp.ones((C, C), dtype=bool))[:, :, None]

    def step(state, inp):
        qt, kt, gt, vt = inp
        b = jnp.cumsum(gt, axis=-2)
        o_inter = jnp.einsum('bhck,bhkv->bhcv', qt * jnp.exp(b), state)
        diff = b[:, :, :, None, :] - b[:, :, None, :, :]
        decay = jnp.exp(jnp.where(causal, diff, -jnp.inf))
        scores = jnp.einsum('bhtk,bhsk,bhtsk->bhts', qt, kt, decay)
        o_intra = jnp.einsum('bhts,bhsv->bhtv', scores, vt)
        b_last = b[:, :, -1:, :]
        new_state = (jnp.exp(b_last[:, :, 0, :])[..., None] * state
                     + jnp.einsum('bhsk,bhsv->bhkv', kt * jnp.exp(b_last - b), vt))
        return new_state, o_inter + o_intra

    s0 = jnp.zeros((B, H, dk, dv), jnp.float32)
    _, o = lax.scan(step, s0, (qc, kc, gc, vc))
    o = o.transpose(1, 0, 3, 2, 4).reshape(B, S, H, dv)
    o = o * lax.rsqrt(jnp.mean(jnp.square(o), axis=-1, keepdims=True) + RMS_EPS)
    o = o.reshape(B, S, H * dv) * g_norm.astype(jnp.float32)
    o = o * jax.nn.silu(gate.astype(jnp.float32))
    return o.astype(q.dtype)


def conv_ffn(h, w_up, conv_w, conv_b, w_down):
    S = h.shape[1]
    u = h @ w_up
    up = jnp.pad(u, ((0, 0), (CONV_WIDTH - 1, 0), (0, 0)))
    uc = conv_b + sum(conv_w[j] * up[:, j:j + S] for j in range(CONV_WIDTH))
    g, val = jnp.split(uc, 2, axis=-1)
    return (jax.nn.silu(g) * val) @ w_down


def setup_inputs(seed: int = 0) -> dict:
    key = jax.random.key(seed)
    ks = jax.random.split(key, 20)
    L = DEPTH
    nrm = lambda k, shape: jax.random.normal(k, shape, jnp.float32)
    return {
        "x": nrm(ks[0], (BATCH, SEQ, D_MODEL)),
        "w_in": nrm(ks[1], (L, D_MODEL, IN_COLS)) * D_MODEL ** -0.5,
        "pool_w": nrm(ks[2], (L, N_POOL_GROUPS, POOL_GROUP_DIM, POOL_GROUP_DIM)) * POOL_GROUP_DIM ** -0.5,
        "pool_b": 0.02 * nrm(ks[3], (L, N_POOL_GROUPS, POOL_GROUP_DIM)),
        "pool_scale": 1.0 + 0.1 * nrm(ks[4], (L, POOL_WIDTH)),
        "hgrn_lb_logits": 1.0 + 0.5 * nrm(ks[5], (L + 1, HGRN_KEY_WIDTH)),
        "hgrn_g_norm": 1.0 + 0.02 * nrm(ks[6], (L, HGRN_WIDTH)),
        "w_out": nrm(ks[7], (L, D_MODEL, D_MODEL)) * D_MODEL ** -0.5 * BETA,
        "ln1_g": 1.0 + 0.02 * nrm(ks[8], (L, D_MODEL)),
        "ln1_b": 0.02 * nrm(ks[9], (L, D_MODEL)),
        "w_up": nrm(ks[10], (L, D_MODEL, 2 * D_FF)) * D_MODEL ** -0.5,
        "conv_w": nrm(ks[11], (L, CONV_WIDTH, 2 * D_FF)) * CONV_WIDTH ** -0.5,
        "conv_b": 0.02 * nrm(ks[12], (L, 2 * D_FF)),
        "w_down": nrm(ks[13], (L, D_FF, D_MODEL)) * D_FF ** -0.5 * BETA,
        "ln2_g": 1.0 + 0.02 * nrm(ks[14], (L, D_MODEL)),
        "ln2_b": 0.02 * nrm(ks[15], (L, D_MODEL)),
    }


def reference(x, w_in, pool_w, pool_b, pool_scale, hgrn_lb_logits, hgrn_g_norm, w_out,
              ln1_g, ln1_b, w_up, conv_w, conv_b, w_down, ln2_g, ln2_b):
    lb_all = jnp.cumsum(jax.nn.softmax(hgrn_lb_logits.astype(jnp.float32), axis=0), axis=0)
    for l in range(DEPTH):
        proj = x @ w_in[l]
        o1 = POOL_WIDTH
        o2 = o1 + HGRN_KEY_WIDTH
        o3 = o2 + HGRN_KEY_WIDTH
        o4 = o3 + HGRN_WIDTH
        u_pool = proj[..., :o1]
        q, f_logit, v, gate = proj[..., o1:o2], proj[..., o2:o3], proj[..., o3:o4], proj[..., o4:]
        a_out = pool_mixer(u_pool, pool_w[l], pool_b[l], pool_scale[l])
        b_out = hgrn2_mixer(q, f_logit, v, gate, lb_all[l], hgrn_g_norm[l])
        mix = jnp.concatenate([a_out, b_out], axis=-1) @ w_out[l]
        x = layer_norm(ALPHA * x + mix, ln1_g[l], ln1_b[l])
        x = layer_norm(ALPHA * x + conv_ffn(x, w_up[l], conv_w[l], conv_b[l], w_down[l]), ln2_g[l], ln2_b[l])
    return x
```

```python
import numpy as np
import ml_dtypes
import concourse.bass as bass
import concourse.mybir as mybir
from concourse.bass_utils import run_bass_kernel_spmd

F32 = mybir.dt.float32
BF16 = mybir.dt.bfloat16
AF = mybir.ActivationFunctionType
ALU = mybir.AluOpType


class _Op:
    __slots__ = ("eng", "fn", "deps", "is_dma", "sem", "val", "signal", "idx", "final", "inc")

    def __init__(self, eng, fn):
        self.eng = eng
        self.fn = fn
        self.deps = []
        self.is_dma = False
        self.sem = None
        self.val = 0
        self.signal = False
        self.idx = 0
        self.final = False
        self.inc = 16


class Prog:
    ENGS = ("pe", "act", "dve", "pool", "sp")

    def __init__(self, nc):
        self.nc = nc
        self.ops = {e: [] for e in self.ENGS}
        self.recs = {}
        self.esem = {e: nc.alloc_semaphore(name="ctr_" + e) for e in ("pe", "act", "dve", "pool")}
        self.dma_sems = {}
        self.dma_cnt = {}
        self.finals = []

    def sbuf(self, name, shape, dtype):
        return self.nc.alloc_sbuf_tensor("sb_" + name, shape, dtype)

    def psum(self, name, shape, dtype):
        return self.nc.alloc_psum_tensor("ps_" + name, shape, dtype)

    def dma_sem(self, name):
        s = self.nc.alloc_semaphore(name="dma_" + name)
        self.dma_cnt[id(s)] = 0
        return s

    @staticmethod
    def _region(a):
        if isinstance(a, tuple):
            return (a, 0, 1)
        cls = type(a.tensor).__name__
        if cls == "PSumTensorHandle":
            return (a.tensor.name, 0, 2048, True)
        if cls != "SBTensorHandle":
            return None
        es = mybir.dt.size(a.dtype)
        ap = a.ap
        row = ap[0][0]
        off = a.offset % row if row > 0 else a.offset
        ext = 1
        for st, cnt in ap[1:]:
            ext += (cnt - 1) * abs(st)
        return (a.tensor.name, off * es, (off + ext) * es)

    def _access(self, op, reg, is_write, deps):
        name, lo, hi = reg[0], reg[1], reg[2]
        if len(reg) > 3:
            is_write = True
        recs = self.recs.get(name, [])
        new = []
        for r in recs:
            rlo, rhi, rop, rw = r
            if rhi <= lo or hi <= rlo:
                new.append(r)
                continue
            if (rw or is_write) and rop is not op:
                deps.append(rop)
            if is_write and lo <= rlo and rhi <= hi:
                continue
            if (not is_write) and (not rw) and rlo == lo and rhi == hi and rop.eng == op.eng \
                    and not rop.is_dma and not op.is_dma:
                continue
            new.append(r)
        new.append((lo, hi, op, is_write))
        self.recs[name] = new

    def _track(self, op, reads, writes):
        deps = []
        for a in reads:
            reg = self._region(a)
            if reg is not None:
                self._access(op, reg, False, deps)
        for a in writes:
            reg = self._region(a)
            if reg is not None:
                self._access(op, reg, True, deps)
        seen = set()
        for d in deps:
            if d is op or id(d) in seen:
                continue
            seen.add(id(d))
            op.deps.append(d)

    def op(self, eng, fn, reads=(), writes=()):
        o = _Op(eng, fn)
        self._track(o, reads, writes)
        self.ops[eng].append(o)
        return o

    def dma(self, eng, out, in_, sem, reads=(), writes=(), final=False, **kw):
        o = _Op(eng, lambda e: e.dma_start(out=out, in_=in_, **kw))
        o.is_dma = True
        o.sem = sem
        self.dma_cnt[id(sem)] += 16
        o.val = self.dma_cnt[id(sem)]
        o.final = final
        self._track(o, reads, writes)
        self.ops[eng].append(o)
        if final:
            self.finals.append(o)
        return o

    def emit(self):
        nc = self.nc
        for e in self.ENGS:
            for o in self.ops[e]:
                for d in o.deps:
                    if d.is_dma:
                        continue
                    if d.eng == "pe" and o.eng == "pe" and not o.is_dma:
                        continue
                    d.signal = True
        for e in self.ENGS:
            n = 0
            for o in self.ops[e]:
                if o.is_dma:
                    continue
                if o.signal:
                    n += 1
                    o.idx = n
        self.sig_counts = {e: sum(1 for o in self.ops[e] if o.signal) for e in self.ENGS}
        handles = {"pe": "tensor", "act": "scalar", "dve": "vector", "pool": "gpsimd", "sp": "sync"}

        def run(ename, eng):
            waited = {}
            for o in self.ops[ename]:
                for d in o.deps:
                    if d.is_dma:
                        sem, val = d.sem, d.val
                    else:
                        if d.eng == "pe" and ename == "pe" and not o.is_dma:
                            continue
                        sem, val = self.esem[d.eng], d.idx
                    if waited.get(id(sem), 0) >= val:
                        continue
                    waited[id(sem)] = val
                    eng.wait_ge(sem, val)
                ins = o.fn(eng)
                if o.is_dma:
                    ins.then_inc(o.sem, o.inc)
                elif o.signal:
                    ins.then_inc(self.esem[ename], 1)
            if ename == "sp":
                for o in self.finals:
                    if waited.get(id(o.sem), 0) >= o.val:
                        continue
                    waited[id(o.sem)] = o.val
                    eng.wait_ge(o.sem, o.val)

        with nc.Block() as block:
            for ename in self.ENGS:
                def mk(ename):
                    def f(eng):
                        run(ename, eng)
                    return f
                getattr(block, handles[ename])(mk(ename))


D = 2048
NK = 16
SEQ = 4096
TP = 3008
TM = 1088
TO = 1024
DFF = 5632
NFF = 44
NH = 8
ALPHA = 2.0 ** 0.25
LN_EPS = 1e-5
RMS_EPS = 1e-6
WINDOWS = (2, 4, 8, 16)
PRE_TILES = [(0, 512), (512, 1024), (1024, 1536), (1536, 2048), (2048, 2560), (2560, 3008)]
MAIN_TILES = [(0, 512), (512, 1024), (1024, 1088)]
T1 = 1026
FF_TILES = [(0, 342), (342, 684), (684, 1026)]

V_PB, V_PS, V_L0, V_L1, V_GN = 0, 8, 16, 24, 32
V_CW, V_CB = 40, 304
V_G1, V_B1, V_G2, V_B2 = 392, 408, 424, 440
NV = 456

ARENA_BYTES = 200704
STG_OFF = 0
X_OFF = 8192
R_OFF = 43008
TMP_OFF = 112640
PT_OFF = 157696


def build_program(stop_after=None, debug=False):
    nc = bass.Bass("TRN2", target_bir_lowering=False)
    P = Prog(nc)

    def dram(name, shape, dt=F32, kind="ExternalInput"):
        return nc.dram_tensor(name, shape, dt, kind=kind).ap()

    xT_h = dram("xT", [D, SEQ])
    w_in_h = dram("w_in", [D, 5120])
    pool_w_h = dram("pool_w", [1024, 256])
    w_out_h = dram("w_out", [D, D])
    w_up_h = dram("w_up", [D, 2 * DFF])
    w_down_h = dram("w_down", [DFF, D])
    vec_h = dram("vec", [128, NV])
    ident_h = dram("ident", [128, 128], BF16)
    maskT_h = dram("maskT", [64, 512])
    rmask_h = dram("rmask", [1, 512])
    invc_h = dram("invc", [1, 64])
    flag_h = dram("flag", [1, 1])
    outT_h = dram("outT", [D, TO], kind="ExternalOutput")
    x1s_h = dram("x1s", [D, TO], kind="Internal")
    dbg_h = None
    if debug:
        dbg_h = dram("dbg", [128, 16 * TM], BF16, kind="ExternalOutput")

    vec = P.sbuf("vec", [128, NV], F32)
    ident = P.sbuf("ident", [128, 128], BF16)
    maskT = P.sbuf("maskT", [64, 512], F32)
    rmask = P.sbuf("rmask", [128, 512], F32)
    invc = P.sbuf("invc", [128, 64], F32)
    flag = P.sbuf("flag", [128, 1], F32)
    lbt = P.sbuf("lbt", [128, 24], F32)
    Sst = P.sbuf("Sst", [128, NH * 128], F32)
    onesf = P.sbuf("onesf", [128, 128], F32)
    onesd = P.sbuf("onesd", [128, 128], F32)
    arena = P.sbuf("arena", [128, ARENA_BYTES // 4], F32)
    arena_bf = arena.bitcast(BF16)
    banks = [P.psum("bank%d" % i, [128, 512], F32) for i in range(8)]

    def fv(off, n):
        assert off % 4 == 0 and off + 4 * n <= ARENA_BYTES, (off, n)
        return arena[:, off // 4: off // 4 + n]

    def bv(off, n):
        assert off % 2 == 0 and off + 2 * n <= ARENA_BYTES, (off, n)
        return arena_bf[:, off // 2: off // 2 + n]

    def mm(out, lhsT, rhs, start=True, stop=True):
        P.op("pe", lambda e: e.matmul(out, lhsT, rhs, start=start, stop=stop),
             reads=[lhsT, rhs], writes=[out])

    def tr(out, in_):
        P.op("pe", lambda e: e.transpose(out, in_, ident[:]), reads=[in_, ident[:]], writes=[out])

    def act(out, in_, func, bias=None, scale=1.0, extra_reads=()):
        rd = [in_] + list(extra_reads)
        kw = {}
        if bias is not None:
            kw["bias"] = bias
            if not isinstance(bias, (int, float)):
                rd.append(bias)
        if not isinstance(scale, (int, float)):
            rd.append(scale)
        P.op("act", lambda e: e.activation(out, in_, func, scale=scale, **kw), reads=rd, writes=[out])

    def ts(eng, out, in0, s1, s2, op0, op1=None):
        rd = [in0] + [s for s in (s1, s2) if s is not None and not isinstance(s, (int, float))]
        if op1 is None:
            P.op(eng, lambda e: e.tensor_scalar(out, in0, s1, None, op0), reads=rd, writes=[out])
        else:
            P.op(eng, lambda e: e.tensor_scalar(out, in0, s1, s2, op0, op1), reads=rd, writes=[out])

    def stt(out, in0, scalar, in1, op0, op1):
        rd = [in0, in1] + ([] if isinstance(scalar, (int, float)) else [scalar])
        P.op("dve", lambda e: e.scalar_tensor_tensor(out, in0, scalar, in1, op0, op1), reads=rd, writes=[out])

    def tt(eng, out, in0, in1, op):
        P.op(eng, lambda e: e.tensor_tensor(out, in0, in1, op), reads=[in0, in1], writes=[out])

    def cp(eng, out, in_):
        if eng == "act":
            act(out, in_, AF.Copy)
        else:
            P.op(eng, lambda e: e.tensor_copy(out, in_), reads=[in_], writes=[out])

    def fill(pairs, sem, queue="pool"):
        ops_ = [P.dma(queue, dst, src, sem, writes=[dst]) for dst, src in pairs]
        tot = P.dma_cnt[id(sem)]
        for o_ in ops_:
            o_.val = tot

    def kview(h_ap):
        return h_ap.rearrange("(k p) c -> p k c", p=128)

    csem = P.dma_sem("const")
    P.dma("sp", vec[:], vec_h, csem, writes=[vec[:]])
    P.dma("sp", ident[:], ident_h, csem, writes=[ident[:]])
    P.dma("sp", maskT[:], maskT_h, csem, writes=[maskT[:]])
    P.dma("sp", rmask[:], rmask_h.partition_broadcast(128), csem, writes=[rmask[:]])
    P.dma("sp", invc[:], invc_h.partition_broadcast(128), csem, writes=[invc[:]])
    P.dma("sp", flag[:], flag_h.partition_broadcast(128), csem, writes=[flag[:]])
    for o_ in P.ops["sp"]:
        if o_.is_dma and o_.sem is csem:
            o_.val = P.dma_cnt[id(csem)]
    P.op("pool", lambda e: e.memset(onesf[:], 1.0), writes=[onesf[:]])
    P.op("pool", lambda e: e.memset(onesd[:], 1.0 / 128.0), writes=[onesd[:]])
    P.op("pool", lambda e: e.memset(Sst[:], 0.0), writes=[Sst[:]])
    tt("dve", lbt[:, 0:8], vec[:, V_L1:V_L1 + 8], vec[:, V_L0:V_L0 + 8], ALU.subtract)
    act(lbt[:, 0:8], lbt[:, 0:8], AF.Exp)
    ts("dve", lbt[:, 0:8], lbt[:, 0:8], 1.0, None, ALU.add)
    P.op("dve", lambda e: e.reciprocal(lbt[:, 0:8], lbt[:, 0:8]), reads=[lbt[:, 0:8]], writes=[lbt[:, 0:8]])
    ts("dve", lbt[:, 8:16], lbt[:, 0:8], -1.0, 1.0, ALU.mult, ALU.add)
    ts("dve", lbt[:, 16:24], lbt[:, 0:8], -1.0, None, ALU.add)

    pb_i = [0]

    def next_proj():
        b = banks[pb_i[0] % 8]
        pb_i[0] += 1
        return b

    tmp_off = [TMP_OFF]

    def ring(nbytes, copies, kind):
        out = []
        for _ in range(copies):
            out.append(fv(tmp_off[0], nbytes // 4) if kind == "f" else bv(tmp_off[0], nbytes // 2))
            tmp_off[0] += nbytes
        return out

    RA = ring(2048, 2, "f")
    RF = ring(2048, 2, "f")
    RD = ring(2048, 2, "f")
    RB = ring(2048, 3, "f")
    RE = ring(2048, 1, "f")
    RH = ring(2048, 2, "f")
    REL = ring(32, 4, "f")
    RKB = ring(1024, 2, "b")
    RV = ring(1024, 5, "b")
    RKV = [ring(2048, 2, "b"), ring(2048, 2, "b")]
    assert tmp_off[0] <= PT_OFF, tmp_off[0]
    RQ = ring(2048, 2, "f")
    RQR = ring(2048, 4, "f")
    RQB = ring(1024, 3, "b")
    RGR = ring(2048, 2, "f")
    RG = ring(2048, 2, "f")
    RSG = ring(2048, 6, "f")
    RSC = ring(1024, 2, "b")
    RSB = ring(2048, 2, "b")
    assert tmp_off[0] <= ARENA_BYTES, tmp_off[0]
    tmp_off[0] = STG_OFF
    RSQ = ring(2048, 2, "f")
    RL = ring(2048, 1, "f")
    RO = ring(2048, 1, "f")
    assert tmp_off[0] <= X_OFF, tmp_off[0]

    Wf = bv(R_OFF, 16 * 1024).rearrange("p (h k c) -> p h k c", h=8, c=128)
    Wv = bv(R_OFF + 32768, 16 * 1024).rearrange("p (h k c) -> p h k c", h=8, c=128)
    wh_sem = [P.dma_sem("wh%d" % i) for i in range(NH)]
    mixT = bv(R_OFF, 16 * TM).rearrange("p (k t) -> p k t", t=TM)
    WS = [bv(R_OFF + 34816 + 16384 * i, 16 * 512).rearrange("p (k c) -> p k c", c=512) for i in range(2)]
    WS4 = [bv(R_OFF + 34816 + 16384 * i, 16 * 512).rearrange("p (g k c) -> p g k c", g=4, c=128) for i in range(2)]
    ws_sem = [P.dma_sem("ws0"), P.dma_sem("ws1")]
    Xpre = [bv(X_OFF + 16384 * i, 16 * 512).rearrange("p (k t) -> p k t", t=512) for i in range(2)]
    xp_sem = [P.dma_sem("xp0"), P.dma_sem("xp1")]
    Xm = bv(X_OFF, 16 * TM).rearrange("p (k t) -> p k t", t=TM)
    xm_sem = P.dma_sem("xm")
    w_sem = P.dma_sem("wfv")

    step_ctr = [0]
    NSTAGE = 7

    def make_step(h, n, xk, wf, wv, wq=None, wg=None, t0=0, pre=None):
        st = dict(h=h, n=n, xk=xk, wf=wf, wv=wv, wq=wq, wg=wg, t0=t0, i=step_ctr[0],
                  full=wq is not None, pre=pre)
        step_ctr[0] += 1
        return st

    def c3(ap, n):
        return ap[:, 0:n].rearrange("p (c j) -> p c j", j=64)

    proj_state = dict(banks=[0, 1], i=0)

    def pbank():
        bk = banks[proj_state["banks"][proj_state["i"] % len(proj_state["banks"])]]
        proj_state["i"] += 1
        return bk

    def proj(st, w, n):
        bk = pbank()
        for k in range(NK):
            mm(bk[:, 0:n], w(k), st["xk"](k), start=(k == 0), stop=(k == NK - 1))
        return bk

    def R(ringbuf, st):
        return ringbuf[st["i"] % len(ringbuf)]

    def stage0(st):
        h, n, full = st["h"], st["n"], st["full"]
        if st["pre"] is not None:
            st["pre"]()
        tA, tF = R(RA, st), R(RF, st)
        lb_h = lbt[:, h:h + 1]
        oml_h = lbt[:, 8 + h:9 + h]
        Pa = proj(st, st["wf"], n)
        act(tA[:, 0:n], Pa[:, 0:n], AF.Exp, scale=-1.0)
        Pb = proj(st, st["wv"], n)
        act(R(RV, st)[:, 0:n], Pb[:, 0:n], AF.Copy)
        act(tA[:, 0:n], tA[:, 0:n], AF.Ln, bias=1.0)
        act(tA[:, 0:n], tA[:, 0:n], AF.Exp, scale=-1.0)
        act(tF[:, 0:n], tA[:, 0:n], AF.Ln, bias=lb_h, scale=oml_h)
        if full:
            tG = R(RG, st)
            Pc = proj(st, st["wq"], n)
            cp("dve", R(RQR, st)[:, 0:n], Pc[:, 0:n])
            Pd = proj(st, st["wg"], n)
            act(tG[:, 0:n], Pd[:, 0:n], AF.Exp, scale=-1.0)
            cp("dve", R(RGR, st)[:, 0:n], Pd[:, 0:n])
            act(tG[:, 0:n], tG[:, 0:n], AF.Ln, bias=1.0)
            act(tG[:, 0:n], tG[:, 0:n], AF.Exp, scale=-1.0)

    def stage1(st):
        h, n, full = st["h"], st["n"], st["full"]
        tA, tF, tD, tB = R(RA, st), R(RF, st), R(RD, st), R(RB, st)
        P.op("dve", lambda e: e.tensor_tensor_scan(tD[:, 0:n], rmask[:, 0:n], tF[:, 0:n], 0.0, ALU.mult, ALU.add),
             reads=[rmask[:, 0:n], tF[:, 0:n]], writes=[tD[:, 0:n]])
        ts("pool", tB[:, 0:n], tA[:, 0:n], lbt[:, 16 + h:17 + h], lbt[:, 8 + h:9 + h], ALU.mult, ALU.add)
        if full:
            stt(R(RSG, st)[:, 0:n], R(RGR, st)[:, 0:n], vec[:, V_GN + h:V_GN + h + 1], R(RG, st)[:, 0:n],
                ALU.mult, ALU.mult)

    def stage2(st):
        n, full = st["n"], st["full"]
        nch = n // 64
        tD, tE, tH = R(RD, st), R(RE, st), R(RH, st)
        bl = tD[:, 63:n:64]
        tt("pool", c3(tE, n), bl.unsqueeze(2).broadcast_to([128, nch, 64]), c3(tD, n), ALU.subtract)
        act(R(REL, st)[:, 0:nch], bl, AF.Exp)
        act(tH[:, 0:n], tE[:, 0:n], AF.Exp)
        if full:
            act(R(RQ, st)[:, 0:n], tE[:, 0:n], AF.Exp, scale=-1.0)

    def stage3(st):
        n, full = st["n"], st["full"]
        tt("pool", R(RKB, st)[:, 0:n], R(RB, st)[:, 0:n], R(RH, st)[:, 0:n], ALU.mult)
        if full:
            tt("dve", R(RQB, st)[:, 0:n], R(RQR, st)[:, 0:n], R(RQ, st)[:, 0:n], ALU.mult)

    def stage4a(st):
        n = st["n"]
        nch = n // 64
        kbT, vT = R(RKB, st), R(RV, st)
        groups = [(0, min(4, nch))] + ([(4, nch)] if nch > 4 else [])
        for gi, (c0, c1) in enumerate(groups):
            bankT = banks[2 + gi].bitcast(BF16)
            for c in range(c0, c1):
                j = c - c0
                tr(bankT[0:64, j * 128:(j + 1) * 128], kbT[:, 64 * c:64 * c + 64])
                tr(bankT[0:64, 512 + j * 128:512 + (j + 1) * 128], vT[:, 64 * c:64 * c + 64])
        for gi, (c0, c1) in enumerate(groups):
            ng = c1 - c0
            bankT = banks[2 + gi].bitcast(BF16)
            src = bankT[0:64, :].rearrange("p (a b) -> p a b", a=2)[:, :, 0:ng * 128]
            dst = R(RKV[gi], st)[0:64, :].rearrange("p (a b) -> p a b", a=2)[:, :, 0:ng * 128]
            cp("act", dst, src)

    def stage4b(st):
        h, n, full = st["h"], st["n"], st["full"]
        nch = n // 64
        kbT, el = R(RKB, st), R(REL, st)
        Sbf = R(RSB, st)
        Sh = Sst[:, h * 128:(h + 1) * 128]
        groups = [(0, min(4, nch))] + ([(4, nch)] if nch > 4 else [])
        for gi, (c0, c1) in enumerate(groups):
            bankI = banks[4 + gi]
            kv = R(RKV[gi], st)
            for c in range(c0, c1):
                j = c - c0
                mm(bankI[:, j * 128:(j + 1) * 128], kv[0:64, j * 128:(j + 1) * 128],
                   kv[0:64, 512 + j * 128:512 + (j + 1) * 128])
        if full:
            qb = R(RQB, st)
            bankSc = banks[6]
            for c in range(nch):
                mm(bankSc[0:64, 64 * c:64 * c + 64], kbT[:, 64 * c:64 * c + 64], qb[:, 64 * c:64 * c + 64])
        if full:
            tt("dve", R(RSC, st)[0:64, 0:n], bankSc[0:64, 0:n], maskT[0:64, 0:n], ALU.mult)
        for gi, (c0, c1) in enumerate(groups):
            bankI = banks[4 + gi]
            for c in range(c0, c1):
                j = c - c0
                e_c = el[:, c:c + 1]
                if full:
                    ts("dve", Sbf[:, c * 128:(c + 1) * 128], Sh, e_c, None, ALU.mult)
                stt(Sh, Sh, e_c, bankI[:, j * 128:(j + 1) * 128], ALU.mult, ALU.add)

    def stage5a(st):
        n, full = st["n"], st["full"]
        if not full:
            return
        nch = n // 64
        qb, scm, Sbf = R(RQB, st), R(RSC, st), R(RSB, st)
        bankO = banks[7]
        for c in range(nch):
            j = c % 4
            mm(bankO[:, 64 * c:64 * c + 64], Sbf[:, c * 128:(c + 1) * 128], qb[:, 64 * c:64 * c + 64],
               start=True, stop=False)
            mm(bankO[:, 64 * c:64 * c + 64], R(RKV[c // 4], st)[0:64, 512 + j * 128:512 + (j + 1) * 128],
               scm[0:64, 64 * c:64 * c + 64], start=False, stop=True)
        act(R(RSQ, st)[:, 0:n], bankO[:, 0:n], AF.Square)

    def stage5b(st):
        h, n, full, t0 = st["h"], st["n"], st["full"], st["t0"]
        if not full:
            return
        sq, tL, tO = R(RSQ, st), R(RL, st), R(RO, st)
        bankO = banks[7]
        bankM = pbank()
        mm(bankM[:, 0:n], onesd[:], sq[:, 0:n])
        act(tL[:, 0:n], bankM[:, 0:n], AF.Ln, bias=RMS_EPS)
        act(tL[:, 0:n], tL[:, 0:n], AF.Exp, scale=-0.5)
        tt("dve", tO[:, 0:n], bankO[:, 0:n], tL[:, 0:n], ALU.mult)
        tt("dve", mixT[:, 8 + h, t0:t0 + n], tO[:, 0:n], R(RSG, st)[:, 0:n], ALU.mult)

    TICK = [(stage4a, 4), (stage5b, 6), (stage3, 3), (stage2, 2), (stage1, 1), (stage0, 0), (stage4b, 4), (stage5a, 5)]

    def run_steps(steps):
        ns = len(steps)
        for t in range(ns + NSTAGE - 1):
            for fn, j in TICK:
                i = t - j
                if 0 <= i < ns:
                    fn(steps[i])

    def load_pre_tile(ti):
        t0, t1 = PRE_TILES[ti]
        n = t1 - t0
        fill([(Xpre[ti % 2][:, :, 0:n], kview(xT_h[:, t0:t1]))], xp_sem[ti % 2])

    def finish_early():
        osem = P.dma_sem("out")
        P.dma("pool", outT_h[0:128, :], fv(X_OFF, 1024), osem, reads=[fv(X_OFF, 1024)], final=True)
        if debug:
            dsem = P.dma_sem("dbg")
            P.dma("pool", dbg_h, bv(R_OFF, 16 * TM), dsem, reads=[bv(R_OFF, 16 * TM)], final=True)
        P.emit()
        return nc, P

    if stop_after == "consts":
        return finish_early()
    steps = []
    load_pre_tile(0)
    for h in range(NH):
        fill([(Wf[:, h, :, :], kview(w_in_h[:, 2048 + h * 128:2048 + (h + 1) * 128])),
              (Wv[:, h, :, :], kview(w_in_h[:, 3072 + h * 128:3072 + (h + 1) * 128]))], wh_sem[h])
    pre_tiles = PRE_TILES[:1] if stop_after == "A1" else PRE_TILES
    for ti, (t0, t1) in enumerate(pre_tiles):
        n = t1 - t0
        xb = Xpre[ti % 2]
        for h in range(NH):
            pre = None
            if h == 0 and ti + 1 < len(pre_tiles):
                pre = (lambda ti=ti: load_pre_tile(ti + 1))
            steps.append(make_step(
                h, n,
                xk=(lambda k, xb=xb, n=n: xb[:, k, 0:n]),
                wf=(lambda k, h=h: Wf[:, h, k, :]),
                wv=(lambda k, h=h: Wv[:, h, k, :]), pre=pre))
    if stop_after == "A1load":
        return finish_early()
    proj_state["banks"] = [0, 1, 6, 7]
    stepsA = steps

    def b_prologue():
        proj_state["banks"] = [0, 1]
        fill([(Xm[:, 0:8, :], kview(xT_h[0:1024, TP:TP + TM])), (Xm[:, 8:16, :], kview(xT_h[1024:2048, TP:TP + TM]))], xm_sem)
        for half in range(2):
            fill([(WS[half][:, :, :], kview(w_in_h[:, half * 512:(half + 1) * 512]))], ws_sem[half])
        o = PT_OFF
        PADW = 16 + TM
        U0s = [fv(o, PADW), fv(o + 4 * PADW, PADW)]; o += 8 * PADW
        A1s = [fv(o, PADW), fv(o + 4 * PADW, PADW)]; o += 8 * PADW
        A2s = [fv(o, PADW), fv(o + 4 * PADW, PADW)]; o += 8 * PADW
        PB = [bv(o, TM), bv(o + 2 * TM, TM)]; o += 4 * TM
        PW = bv(o, 8 * 256).rearrange("p (r d) -> p r d", d=256); o += 4096
        t16 = fv(o, 16); o += 64
        assert o <= ARENA_BYTES, o
        pw_sem = P.dma_sem("pw")
        fill([(PW[:, :, :], pool_w_h.rearrange("(r p) d -> p r d", p=128))], pw_sem)
        for buf in U0s + A1s + A2s:
            P.op("pool", (lambda e, buf=buf: e.memset(buf[:, 0:16], 0.0)), writes=[buf[:, 0:16]])

        for cc in range(8):
            g = cc // 2
            w = WINDOWS[g]
            ws = WS[cc // 4]
            col = (cc % 4) * 128
            U0, A1, A2 = U0s[cc % 2], A1s[cc % 2], A2s[cc % 2]
            for (t0, t1) in MAIN_TILES:
                n = t1 - t0
                bk = next_proj()
                for k in range(NK):
                    mm(bk[:, 0:n], ws[:, k, col:col + 128], Xm[:, k, t0:t1], start=(k == 0), stop=(k == NK - 1))
                act(U0[:, 16 + t0:16 + t1], bk[:, 0:n], AF.Copy)
            cur = U0
            seq = [A1, A2, A1, A2]
            sh = 1
            i = 0
            while sh < w:
                nxt = seq[i]
                tt("dve", nxt[:, 16:16 + TM], cur[:, 16:16 + TM], cur[:, 16 - sh:16 - sh + TM], ALU.add)
                cur = nxt
                sh *= 2
                i += 1
            pb = PB[cc % 2]
            stt(pb[:, 0:TM], cur[:, 16:16 + TM], 1.0 / w, U0[:, 16:16 + TM], ALU.mult, ALU.subtract)
            tt("dve", t16[:, 0:16], cur[:, 16 + 64:16 + 80], invc[:, g * 16:(g + 1) * 16], ALU.mult)
            tt("dve", pb[:, 64:80], t16[:, 0:16], U0[:, 16 + 64:16 + 80], ALU.subtract)
            if cc % 2 == 1:
                for dc in range(2):
                    oc = g * 2 + dc
                    for (t0, t1) in MAIN_TILES:
                        n = t1 - t0
                        bk = next_proj()
                        for ci in range(2):
                            mm(bk[:, 0:n], PW[:, g * 2 + ci, dc * 128:(dc + 1) * 128], PB[ci][:, t0:t1],
                               start=(ci == 0), stop=(ci == 1))
                        ts("dve", mixT[:, oc, t0:t1], bk[:, 0:n], vec[:, V_PB + oc:V_PB + oc + 1],
                           vec[:, V_PS + oc:V_PS + oc + 1], ALU.add, ALU.mult)

        load_head(0)

    if stop_after == "poolmix":
        return finish_early()

    def load_head(h):
        slot = WS4[h % 2]
        fill([(slot[:, gi, :, :], kview(w_in_h[:, base + h * 128:base + (h + 1) * 128]))
              for gi, base in enumerate((1024, 2048, 3072, 4096))], ws_sem[h % 2])

    if stop_after == "B1load":
        load_head(0)
        mm(banks[0][:, 0:512], WS4[0][:, 0, 0, :], Xm[:, 0, 0:512])
        return finish_early()
    steps = []
    nheads = 1 if stop_after in ("B1", "B1a") else NH
    for h in range(nheads):
        slot = WS4[h % 2]
        for tix, (t0, t1) in enumerate(MAIN_TILES[:1] if stop_after == "B1a" else MAIN_TILES):
            n = t1 - t0
            pre = None
            if tix == 0 and h + 1 < nheads:
                pre = (lambda h=h: load_head(h + 1))
            if tix == 0 and h == 0:
                pre = (lambda: (b_prologue(), load_head(1)))
            steps.append(make_step(
                h, n,
                xk=(lambda k, t0=t0, t1=t1: Xm[:, k, t0:t1]),
                wq=(lambda k, slot=slot: slot[:, 0, k, :]),
                wf=(lambda k, slot=slot: slot[:, 1, k, :]),
                wv=(lambda k, slot=slot: slot[:, 2, k, :]),
                wg=(lambda k, slot=slot: slot[:, 3, k, :]),
                t0=t0, pre=pre))
    run_steps(stepsA + steps)

    if stop_after in ("mixer", "B1", "B1a"):
        return finish_early()

    Z1_OFFS = [X_OFF + 4352 * d for d in range(8)] + [TMP_OFF + 4352 * d for d in range(8)]
    z1 = [fv(Z1_OFFS[d], TM) for d in range(16)]
    o = TMP_OFF + 8 * 4352
    XR = [fv(o, TM), fv(o + 4352, TM)]; o += 8704
    XF = [fv(o, TM), fv(o + 4352, TM)]; o += 8704
    sqx = [fv(o, 512), fv(o + 2048, 512)]; o += 4096
    assert o <= ARENA_BYTES, o
    xr_sem = [P.dma_sem("xr0"), P.dma_sem("xr1")]
    mu = fv(R_OFF + 34816, TM)
    rs = fv(R_OFF + 34816 + 4352, TM)
    msq = fv(R_OFF + 34816 + 8704, TM)
    x1T = bv(R_OFF, 16 * T1).rearrange("p (k t) -> p k t", t=T1)

    def load_wout(g):
        fill([(WS[g % 2][:, :, :], kview(w_out_h[:, g * 512:(g + 1) * 512]))], ws_sem[g % 2])

    acc1 = fv(o, TM); o += 4 * TM
    acc2 = fv(o, TM); o += 4 * TM
    assert o <= ARENA_BYTES, o
    sq_i = [0]

    def stat_acc(zap, a1, a2, first, e1="pool", e2="pool"):
        sb = sqx[sq_i[0] % 2]
        sq_i[0] += 1
        n_ = zap.shape[1]
        if first:
            cp(e1, a1, zap)
            act(a2, zap, AF.Square)
        else:
            act(sb[:, 0:n_], zap, AF.Square)
            tt(e1, a1, a1, zap, ALU.add)
            tt(e2, a2, a2, sb[:, 0:n_], ALU.add)

    def ln_finish(a1, a2, tiles, mu_, rs_, msq_, eps):
        for (t0, t1) in tiles:
            n = t1 - t0
            b0 = next_proj()
            b1 = next_proj()
            mm(b0[:, 0:n], onesf[:], a1[:, t0:t1])
            mm(b1[:, 0:n], onesf[:], a2[:, t0:t1])
            ts("dve", mu_[:, t0:t1], b0[:, 0:n], 1.0 / D, None, ALU.mult)
            tt("dve", msq_[:, t0:t1], mu_[:, t0:t1], mu_[:, t0:t1], ALU.mult)
            stt(msq_[:, t0:t1], b1[:, 0:n], 1.0 / D, msq_[:, t0:t1], ALU.mult, ALU.subtract)
            act(rs_[:, t0:t1], msq_[:, t0:t1], AF.Ln, bias=eps)
            act(rs_[:, t0:t1], rs_[:, t0:t1], AF.Exp, scale=-0.5)

    load_wout(0)
    for g in range(4):
        if g + 1 < 4:
            load_wout(g + 1)
        for dc in range(4):
            d = 4 * g + dc
            xr = XR[d % 2]
            P.dma("sp", xr[:, 0:TM], xT_h[d * 128:(d + 1) * 128, TP:TP + TM], xr_sem[d % 2], writes=[xr[:, 0:TM]])
            for (t0, t1) in MAIN_TILES:
                n = t1 - t0
                bk = next_proj()
                for k in range(NK):
                    mm(bk[:, 0:n], WS[g % 2][:, k, dc * 128:(dc + 1) * 128], mixT[:, k, t0:t1],
                       start=(k == 0), stop=(k == NK - 1))
                stt(z1[d][:, t0:t1], xr[:, t0:t1], ALPHA, bk[:, 0:n], ALU.mult, ALU.add)
                stat_acc(z1[d][:, t0:t1], acc1[:, t0:t1], acc2[:, t0:t1], d == 0, e1="dve", e2="dve")

    ln_finish(acc1, acc2, MAIN_TILES, mu, rs, msq, LN_EPS)
    x1_sem = [P.dma_sem("x1s0"), P.dma_sem("x1s1")]
    WD_OFF = ARENA_BYTES - 16384
    assert o <= WD_OFF
    WD = [bv(WD_OFF + 4096 * i, 16 * 128).rearrange("p (k c) -> p k c", c=128) for i in range(4)]
    wd_sem = [P.dma_sem("wd0"), P.dma_sem("wd1")]

    def load_wup(p):
        fill([(WD[(p % 2) * 2 + half][:, :, :], kview(w_up_h[:, half * DFF + p * 128:half * DFF + (p + 1) * 128]))
              for half in range(2)], wd_sem[p % 2])

    load_wup(0)
    load_wup(1)
    nb1 = msq
    stt(nb1[:, 0:TM], mu[:, 0:TM], -1.0, rs[:, 0:TM], ALU.mult, ALU.mult)
    for d in range(16):
        tt("dve", z1[d][:, 0:TM], z1[d][:, 0:TM], rs[:, 0:TM], ALU.mult)
        tt("dve", z1[d][:, 0:TM], z1[d][:, 0:TM], nb1[:, 0:TM], ALU.add)
        act(x1T[:, d, 0:T1], z1[d][:, 62:TM], AF.Identity, bias=vec[:, V_B1 + d:V_B1 + d + 1],
            scale=vec[:, V_G1 + d:V_G1 + d + 1])
        ts("dve", x1T[:, d, 0:2], x1T[:, d, 0:2], flag[:, 0:1], None, ALU.mult)
    XF4 = [XF[0], XF[1], XR[0], XR[1]]
    x1_sem4 = x1_sem + [P.dma_sem("x1s2"), P.dma_sem("x1s3")]
    for d in range(16):
        xf = XF4[d % 4]
        ts("dve", xf[:, 0:TO], z1[d][:, 64:TM], vec[:, V_G1 + d:V_G1 + d + 1], vec[:, V_B1 + d:V_B1 + d + 1],
           ALU.mult, ALU.add)
        P.dma("sp", x1s_h[d * 128:(d + 1) * 128, :], xf[:, 0:TO], x1_sem4[d % 4],
              reads=[xf[:, 0:TO]], writes=[("x1s", d)])

    if stop_after == "ln1":
        if debug:
            dsem = P.dma_sem("dbg")
            P.dma("pool", dbg_h[:, 0:16 * T1], bv(R_OFF, 16 * T1), dsem, reads=[bv(R_OFF, 16 * T1)], final=True)
        osem = P.dma_sem("out")
        P.dma("pool", outT_h[0:128, :], fv(X_OFF, 1024), osem, reads=[fv(X_OFF, 1024)], final=True)
        P.emit()
        return nc, P

    HT_OFF = R_OFF + 2 * 16 * T1
    assert HT_OFF + NFF * TO * 2 <= ARENA_BYTES - 4112, HT_OFF
    hT = bv(HT_OFF, NFF * TO).rearrange("p (k t) -> p k t", t=TO)
    o = X_OFF
    UR = [fv(o, 1028), fv(o + 4112, 1028)]; o += 8224
    ACC = [fv(o, 1028), fv(o + 4112, 1028)]; o += 8224
    assert o <= R_OFF, o
    SG = fv(HT_OFF + NFF * TO * 2, 1028)
    assert HT_OFF + NFF * TO * 2 + 4112 <= ARENA_BYTES

    for p in range(NFF):
        if 2 <= p + 1 < NFF:
            load_wup(p + 1)
        for half in range(2):
            wd = WD[(p % 2) * 2 + half]
            cidx = half * NFF + p
            bks = [next_proj() for _ in FF_TILES]
            for k in range(NK):
                for ti, (t0, t1) in enumerate(FF_TILES):
                    mm(bks[ti][:, 0:t1 - t0], wd[:, k, :], x1T[:, k, t0:t1], start=(k == 0), stop=(k == NK - 1))
            ur = UR[half]
            for ti, (t0, t1) in enumerate(FF_TILES):
                act(ur[:, t0:t1], bks[ti][:, 0:t1 - t0], AF.Copy)
            acc = ACC[half]
            cw = lambda j: vec[:, V_CW + 88 * j + cidx:V_CW + 88 * j + cidx + 1]
            ts("dve", acc[:, 0:TO], ur[:, 2:2 + TO], cw(2), vec[:, V_CB + cidx:V_CB + cidx + 1], ALU.mult, ALU.add)
            stt(acc[:, 0:TO], ur[:, 1:1 + TO], cw(1), acc[:, 0:TO], ALU.mult, ALU.add)
            stt(acc[:, 0:TO], ur[:, 0:TO], cw(0), acc[:, 0:TO], ALU.mult, ALU.add)
            if half == 0:
                act(SG[:, 0:TO], acc[:, 0:TO], AF.Silu)
            else:
                tt("dve", hT[:, p, :], SG[:, 0:TO], acc[:, 0:TO], ALU.mult)

    NWR = 8
    WR_OFF = X_OFF + 4096 * 14
    Z2_OFFS = [X_OFF + 4096 * d for d in range(14)] + [HT_OFF + 4096 * d for d in range(2)]
    z2 = [fv(Z2_OFFS[d], TO) for d in range(16)]
    WR = [bv(WR_OFF + 1024 * i, 2 * 256).rearrange("p (a c) -> p a c", c=256) for i in range(NWR)]
    assert WR_OFF + 1024 * NWR <= HT_OFF
    XS = [fv(STG_OFF, 1024), fv(STG_OFF + 4096, 1024)]
    wr_sem = [P.dma_sem("wr%d" % i) for i in range(NWR)]
    xs_sem = [P.dma_sem("xs0"), P.dma_sem("xs1")]
    OF = [fv(HT_OFF + 8192, TO), fv(HT_OFF + 8192 + 4096, TO), fv(HT_OFF + 28672, TO), fv(HT_OFF + 32768, TO)]
    of_sem = [P.dma_sem("of%d" % i) for i in range(4)]
    sqx = [fv(HT_OFF + NFF * TO * 2, 512), fv(HT_OFF + NFF * TO * 2 + 2048, 512)]
    acc1 = fv(HT_OFF + NFF * TO * 2 + 4096, TO)
    acc2 = fv(HT_OFF + NFF * TO * 2 + 8192, TO)
    assert HT_OFF + NFF * TO * 2 + 12288 <= ARENA_BYTES
    mu2 = fv(HT_OFF + 16384, TO)
    rs2 = fv(HT_OFF + 16384 + 4096, TO)
    msq2 = fv(HT_OFF + 16384 + 8192, TO)
    E_TILES = [(0, 512), (512, 1024)]
    wr_i = 0
    for g in range(8):
        accb = [banks[(g % 2) * 4 + i] for i in range(4)]
        for k2 in range(NFF // 2):
            slot = WR[wr_i % NWR]
            src = w_down_h[k2 * 256:(k2 + 1) * 256, g * 256:(g + 1) * 256].rearrange("(a p) c -> p a c", p=128)
            fill([(slot[:, :, :], src)], wr_sem[wr_i % NWR])
            wr_i += 1
            for a in range(2):
                k = 2 * k2 + a
                for dc in range(2):
                    for ti, (t0, t1) in enumerate(E_TILES):
                        mm(accb[dc * 2 + ti][:, 0:512], slot[:, a, dc * 128:(dc + 1) * 128], hT[:, k, t0:t1],
                           start=(k == 0), stop=(k == NFF - 1))
        for dc in range(2):
            d = 2 * g + dc
            xs = XS[d % 2]
            P.dma("sp", xs[:, 0:TO], x1s_h[d * 128:(d + 1) * 128, :], xs_sem[d % 2],
                  reads=[("x1s", d)], writes=[xs[:, 0:TO]])
            for ti, (t0, t1) in enumerate(E_TILES):
                stt(z2[d][:, t0:t1], xs[:, t0:t1], ALPHA, accb[dc * 2 + ti][:, 0:512], ALU.mult, ALU.add)
        for dc in range(2):
            d = 2 * g + dc
            for ti, (t0, t1) in enumerate(E_TILES):
                stat_acc(z2[d][:, t0:t1], acc1[:, t0:t1], acc2[:, t0:t1], d == 0, e1="dve", e2="dve")
    ln_finish(acc1, acc2, E_TILES, mu2, rs2, msq2, LN_EPS)
    nb2 = msq2
    stt(nb2[:, 0:TO], mu2[:, 0:TO], -1.0, rs2[:, 0:TO], ALU.mult, ALU.mult)
    for d in range(16):
        tt("dve", z2[d][:, 0:TO], z2[d][:, 0:TO], rs2[:, 0:TO], ALU.mult)
        tt("dve", z2[d][:, 0:TO], z2[d][:, 0:TO], nb2[:, 0:TO], ALU.add)
        of = OF[d % 4]
        act(of[:, 0:TO], z2[d][:, 0:TO], AF.Identity, bias=vec[:, V_B2 + d:V_B2 + d + 1],
            scale=vec[:, V_G2 + d:V_G2 + d + 1])
        P.dma("sp", outT_h[d * 128:(d + 1) * 128, :], of[:, 0:TO], of_sem[d % 4], reads=[of[:, 0:TO]], final=True)
    P.emit()
    return nc, P


def _cols(v, n):
    return np.ascontiguousarray(np.asarray(v, np.float32).reshape(n, 128).T)


def prep_inputs(x, w_in, pool_w, pool_b, pool_scale, hgrn_lb_logits, hgrn_g_norm, w_out,
                ln1_g, ln1_b, w_up, conv_w, conv_b, w_down, ln2_g, ln2_b):
    f32 = lambda a: np.ascontiguousarray(np.asarray(a, np.float32))
    x = f32(x)
    vec = np.zeros((128, NV), np.float32)
    vec[:, V_PB:V_PB + 8] = _cols(np.asarray(pool_b)[0].reshape(-1), 8)
    vec[:, V_PS:V_PS + 8] = _cols(np.asarray(pool_scale)[0], 8)
    vec[:, V_L0:V_L0 + 8] = _cols(np.asarray(hgrn_lb_logits)[0], 8)
    vec[:, V_L1:V_L1 + 8] = _cols(np.asarray(hgrn_lb_logits)[1], 8)
    vec[:, V_GN:V_GN + 8] = _cols(np.asarray(hgrn_g_norm)[0], 8)
    cw = np.asarray(conv_w, np.float32)[0]
    for j in range(3):
        vec[:, V_CW + 88 * j:V_CW + 88 * (j + 1)] = _cols(cw[j], 88)
    vec[:, V_CB:V_CB + 88] = _cols(np.asarray(conv_b)[0], 88)
    vec[:, V_G1:V_G1 + 16] = _cols(np.asarray(ln1_g)[0], 16)
    vec[:, V_B1:V_B1 + 16] = _cols(np.asarray(ln1_b)[0], 16)
    vec[:, V_G2:V_G2 + 16] = _cols(np.asarray(ln2_g)[0], 16)
    vec[:, V_B2:V_B2 + 16] = _cols(np.asarray(ln2_b)[0], 16)
    ident = np.eye(128, dtype=np.float32).astype(ml_dtypes.bfloat16)
    m = np.triu(np.ones((64, 64), np.float32))
    maskT = np.ascontiguousarray(np.tile(m, (1, 8)))
    rmask = np.ones((1, 512), np.float32)
    rmask[0, ::64] = 0.0
    shared = dict(
        w_in=f32(w_in)[0], pool_w=f32(pool_w)[0].reshape(1024, 256), w_out=f32(w_out)[0],
        w_up=f32(w_up)[0], w_down=f32(w_down)[0], vec=vec, ident=ident, maskT=maskT, rmask=rmask)
    in_maps = []
    for c in range(8):
        b, j = divmod(c, 4)
        npre = 1024 * (j + 1)
        xcat = np.zeros((SEQ, D), np.float32)
        xcat[SEQ - npre:] = x[b, :npre]
        invc = np.zeros((1, 64), np.float32)
        for g, w in enumerate(WINDOWS):
            for i in range(16):
                invc[0, g * 16 + i] = 1.0 / min(1024 * j + i + 1, w)
        flag = np.full((1, 1), 0.0 if j == 0 else 1.0, np.float32)
        m_ = dict(shared)
        m_.update(xT=np.ascontiguousarray(xcat.T), invc=invc, flag=flag)
        in_maps.append(m_)
    return in_maps


def kernel(**inputs):
    in_maps = prep_inputs(**inputs)
    nc, _ = build_program()
    res = run_bass_kernel_spmd(nc, in_maps, core_ids=list(range(8)))
    out = np.zeros((2, SEQ, D), np.float32)
    for c in range(8):
        b, j = divmod(c, 4)
        out[b, 1024 * j:1024 * (j + 1), :] = res.results[c]["outT"].T
    return out
```

```python
import numpy as np
import ml_dtypes
import concourse.bass as bass
import concourse.mybir as mybir
from concourse.bass_utils import run_bass_kernel_spmd

F32 = mybir.dt.float32
BF16 = mybir.dt.bfloat16
AF = mybir.ActivationFunctionType
ALU = mybir.AluOpType


class _Op:
    __slots__ = ("eng", "fn", "deps", "is_dma", "sem", "val", "signal", "idx", "final", "inc")

    def __init__(self, eng, fn):
        self.eng = eng
        self.fn = fn
        self.deps = []
        self.is_dma = False
        self.sem = None
        self.val = 0
        self.signal = False
        self.idx = 0
        self.final = False
        self.inc = 16


class Prog:
    ENGS = ("pe", "act", "dve", "pool", "sp")

    def __init__(self, nc):
        self.nc = nc
        self.ops = {e: [] for e in self.ENGS}
        self.recs = {}
        self.esem = {e: nc.alloc_semaphore(name="ctr_" + e) for e in ("pe", "act", "dve", "pool")}
        self.dma_sems = {}
        self.dma_cnt = {}
        self.finals = []

    def sbuf(self, name, shape, dtype):
        return self.nc.alloc_sbuf_tensor("sb_" + name, shape, dtype)

    def psum(self, name, shape, dtype):
        return self.nc.alloc_psum_tensor("ps_" + name, shape, dtype)

    def dma_sem(self, name):
        s = self.nc.alloc_semaphore(name="dma_" + name)
        self.dma_cnt[id(s)] = 0
        return s

    @staticmethod
    def _region(a):
        if isinstance(a, tuple):
            return (a, 0, 1)
        cls = type(a.tensor).__name__
        if cls == "PSumTensorHandle":
            return (a.tensor.name, 0, 2048, True)
        if cls != "SBTensorHandle":
            return None
        es = mybir.dt.size(a.dtype)
        ap = a.ap
        row = ap[0][0]
        off = a.offset % row if row > 0 else a.offset
        ext = 1
        for st, cnt in ap[1:]:
            ext += (cnt - 1) * abs(st)
        return (a.tensor.name, off * es, (off + ext) * es)

    def _access(self, op, reg, is_write, deps):
        name, lo, hi = reg[0], reg[1], reg[2]
        if len(reg) > 3:
            is_write = True
        recs = self.recs.get(name, [])
        new = []
        for r in recs:
            rlo, rhi, rop, rw = r
            if rhi <= lo or hi <= rlo:
                new.append(r)
                continue
            if (rw or is_write) and rop is not op:
                deps.append(rop)
            if is_write and lo <= rlo and rhi <= hi:
                continue
            if (not is_write) and (not rw) and rlo == lo and rhi == hi and rop.eng == op.eng \
                    and not rop.is_dma and not op.is_dma:
                continue
            new.append(r)
        new.append((lo, hi, op, is_write))
        self.recs[name] = new

    def _track(self, op, reads, writes):
        deps = []
        for a in reads:
            reg = self._region(a)
            if reg is not None:
                self._access(op, reg, False, deps)
        for a in writes:
            reg = self._region(a)
            if reg is not None:
                self._access(op, reg, True, deps)
        seen = set()
        for d in deps:
            if d is op or id(d) in seen:
                continue
            seen.add(id(d))
            op.deps.append(d)

    def op(self, eng, fn, reads=(), writes=()):
        o = _Op(eng, fn)
        self._track(o, reads, writes)
        self.ops[eng].append(o)
        return o

    def dma(self, eng, out, in_, sem, reads=(), writes=(), final=False, **kw):
        o = _Op(eng, lambda e: e.dma_start(out=out, in_=in_, **kw))
        o.is_dma = True
        o.sem = sem
        self.dma_cnt[id(sem)] += 16
        o.val = self.dma_cnt[id(sem)]
        o.final = final
        self._track(o, reads, writes)
        self.ops[eng].append(o)
        if final:
            self.finals.append(o)
        return o

    def emit(self):
        nc = self.nc
        for e in self.ENGS:
            for o in self.ops[e]:
                for d in o.deps:
                    if d.is_dma:
                        continue
                    if d.eng == "pe" and o.eng == "pe" and not o.is_dma:
                        continue
                    d.signal = True
        for e in self.ENGS:
            n = 0
            for o in self.ops[e]:
                if o.is_dma:
                    continue
                if o.signal:
                    n += 1
                    o.idx = n
        self.sig_counts = {e: sum(1 for o in self.ops[e] if o.signal) for e in self.ENGS}
        handles = {"pe": "tensor", "act": "scalar", "dve": "vector", "pool": "gpsimd", "sp": "sync"}

        def run(ename, eng):
            waited = {}
            for o in self.ops[ename]:
                for d in o.deps:
                    if d.is_dma:
                        sem, val = d.sem, d.val
                    else:
                        if d.eng == "pe" and ename == "pe" and not o.is_dma:
                            continue
                        sem, val = self.esem[d.eng], d.idx
                    if waited.get(id(sem), 0) >= val:
                        continue
                    waited[id(sem)] = val
                    eng.wait_ge(sem, val)
                ins = o.fn(eng)
                if o.is_dma:
                    ins.then_inc(o.sem, o.inc)
                elif o.signal:
                    ins.then_inc(self.esem[ename], 1)
            if ename == "sp":
                for o in self.finals:
                    if waited.get(id(o.sem), 0) >= o.val:
                        continue
                    waited[id(o.sem)] = o.val
                    eng.wait_ge(o.sem, o.val)

        with nc.Block() as block:
            for ename in self.ENGS:
                def mk(ename):
                    def f(eng):
                        run(ename, eng)
                    return f
                getattr(block, handles[ename])(mk(ename))


D = 2048
NK = 16
SEQ = 4096
TP = 3008
TM = 1088
TO = 1024
DFF = 5632
NFF = 44
NH = 8
ALPHA = 2.0 ** 0.25
LN_EPS = 1e-5
RMS_EPS = 1e-6
WINDOWS = (2, 4, 8, 16)
PRE_TILES = [(0, 512), (512, 1024), (1024, 1536), (1536, 2048), (2048, 2560), (2560, 3008)]
MAIN_TILES = [(0, 512), (512, 1024), (1024, 1088)]
T1 = 1026
FF_TILES = [(0, 342), (342, 684), (684, 1026)]

V_PB, V_PS, V_L0, V_L1, V_GN = 0, 8, 16, 24, 32
V_CW, V_CB = 40, 304
V_G1, V_B1, V_G2, V_B2 = 392, 408, 424, 440
NV = 456

ARENA_BYTES = 200704
STG_OFF = 0
X_OFF = 8192
R_OFF = 43008
TMP_OFF = 112640
PT_OFF = 157696


def build_program(stop_after=None, debug=False):
    nc = bass.Bass("TRN2", target_bir_lowering=False)
    P = Prog(nc)

    def dram(name, shape, dt=F32, kind="ExternalInput"):
        return nc.dram_tensor(name, shape, dt, kind=kind).ap()

    xT_h = dram("xT", [D, SEQ])
    w_in_h = dram("w_in", [D, 5120])
    pool_w_h = dram("pool_w", [1024, 256])
    w_out_h = dram("w_out", [D, D])
    w_up_h = dram("w_up", [D, 2 * DFF])
    w_down_h = dram("w_down", [DFF, D])
    vec_h = dram("vec", [128, NV])
    ident_h = dram("ident", [128, 128], BF16)
    maskT_h = dram("maskT", [64, 512])
    rmask_h = dram("rmask", [1, 512])
    invc_h = dram("invc", [1, 64])
    flag_h = dram("flag", [1, 1])
    outT_h = dram("outT", [D, TO], kind="ExternalOutput")
    x1s_h = dram("x1s", [D, TO], kind="Internal")
    dbg_h = None
    if debug:
        dbg_h = dram("dbg", [128, 16 * TM], BF16, kind="ExternalOutput")

    vec = P.sbuf("vec", [128, NV], F32)
    ident = P.sbuf("ident", [128, 128], BF16)
    maskT = P.sbuf("maskT", [64, 512], F32)
    rmask = P.sbuf("rmask", [128, 512], F32)
    invc = P.sbuf("invc", [128, 64], F32)
    flag = P.sbuf("flag", [128, 1], F32)
    lbt = P.sbuf("lbt", [128, 24], F32)
    Sst = P.sbuf("Sst", [128, NH * 128], F32)
    onesf = P.sbuf("onesf", [128, 128], F32)
    onesd = P.sbuf("onesd", [128, 128], F32)
    arena = P.sbuf("arena", [128, ARENA_BYTES // 4], F32)
    arena_bf = arena.bitcast(BF16)
    banks = [P.psum("bank%d" % i, [128, 512], F32) for i in range(8)]

    def fv(off, n):
        assert off % 4 == 0 and off + 4 * n <= ARENA_BYTES, (off, n)
        return arena[:, off // 4: off // 4 + n]

    def bv(off, n):
        assert off % 2 == 0 and off + 2 * n <= ARENA_BYTES, (off, n)
        return arena_bf[:, off // 2: off // 2 + n]

    def mm(out, lhsT, rhs, start=True, stop=True):
        P.op("pe", lambda e: e.matmul(out, lhsT, rhs, start=start, stop=stop),
             reads=[lhsT, rhs], writes=[out])

    def tr(out, in_):
        P.op("pe", lambda e: e.transpose(out, in_, ident[:]), reads=[in_, ident[:]], writes=[out])

    def act(out, in_, func, bias=None, scale=1.0, extra_reads=()):
        rd = [in_] + list(extra_reads)
        kw = {}
        if bias is not None:
            kw["bias"] = bias
            if not isinstance(bias, (int, float)):
                rd.append(bias)
        if not isinstance(scale, (int, float)):
            rd.append(scale)
        P.op("act", lambda e: e.activation(out, in_, func, scale=scale, **kw), reads=rd, writes=[out])

    def ts(eng, out, in0, s1, s2, op0, op1=None):
        rd = [in0] + [s for s in (s1, s2) if s is not None and not isinstance(s, (int, float))]
        if op1 is None:
            P.op(eng, lambda e: e.tensor_scalar(out, in0, s1, None, op0), reads=rd, writes=[out])
        else:
            P.op(eng, lambda e: e.tensor_scalar(out, in0, s1, s2, op0, op1), reads=rd, writes=[out])

    def stt(out, in0, scalar, in1, op0, op1):
        rd = [in0, in1] + ([] if isinstance(scalar, (int, float)) else [scalar])
        P.op("dve", lambda e: e.scalar_tensor_tensor(out, in0, scalar, in1, op0, op1), reads=rd, writes=[out])

    def tt(eng, out, in0, in1, op):
        P.op(eng, lambda e: e.tensor_tensor(out, in0, in1, op), reads=[in0, in1], writes=[out])

    def cp(eng, out, in_):
        if eng == "act":
            act(out, in_, AF.Copy)
        else:
            P.op(eng, lambda e: e.tensor_copy(out, in_), reads=[in_], writes=[out])

    def fill(pairs, sem, queue="pool"):
        ops_ = [P.dma(queue, dst, src, sem, writes=[dst]) for dst, src in pairs]
        tot = P.dma_cnt[id(sem)]
        for o_ in ops_:
            o_.val = tot

    def kview(h_ap):
        return h_ap.rearrange("(k p) c -> p k c", p=128)

    csem = P.dma_sem("const")
    P.dma("sp", vec[:], vec_h, csem, writes=[vec[:]])
    P.dma("sp", ident[:], ident_h, csem, writes=[ident[:]])
    P.dma("sp", maskT[:], maskT_h, csem, writes=[maskT[:]])
    P.dma("sp", rmask[:], rmask_h.partition_broadcast(128), csem, writes=[rmask[:]])
    P.dma("sp", invc[:], invc_h.partition_broadcast(128), csem, writes=[invc[:]])
    P.dma("sp", flag[:], flag_h.partition_broadcast(128), csem, writes=[flag[:]])
    for o_ in P.ops["sp"]:
        if o_.is_dma and o_.sem is csem:
            o_.val = P.dma_cnt[id(csem)]
    P.op("pool", lambda e: e.memset(onesf[:], 1.0), writes=[onesf[:]])
    P.op("pool", lambda e: e.memset(onesd[:], 1.0 / 128.0), writes=[onesd[:]])
    P.op("pool", lambda e: e.memset(Sst[:], 0.0), writes=[Sst[:]])
    tt("dve", lbt[:, 0:8], vec[:, V_L1:V_L1 + 8], vec[:, V_L0:V_L0 + 8], ALU.subtract)
    act(lbt[:, 0:8], lbt[:, 0:8], AF.Exp)
    ts("dve", lbt[:, 0:8], lbt[:, 0:8], 1.0, None, ALU.add)
    P.op("dve", lambda e: e.reciprocal(lbt[:, 0:8], lbt[:, 0:8]), reads=[lbt[:, 0:8]], writes=[lbt[:, 0:8]])
    ts("dve", lbt[:, 8:16], lbt[:, 0:8], -1.0, 1.0, ALU.mult, ALU.add)
    ts("dve", lbt[:, 16:24], lbt[:, 0:8], -1.0, None, ALU.add)
    gfb = P.sbuf("gfb", [128, 32], F32)
    ts("dve", gfb[:, 0:16], vec[:, V_G1:V_G1 + 16], flag[:, 0:1], None, ALU.mult)
    ts("dve", gfb[:, 16:32], vec[:, V_B1:V_B1 + 16], flag[:, 0:1], None, ALU.mult)

    pb_i = [0]

    def next_proj():
        b = banks[pb_i[0] % 8]
        pb_i[0] += 1
        return b

    tmp_off = [TMP_OFF]

    def ring(nbytes, copies, kind):
        out = []
        for _ in range(copies):
            out.append(fv(tmp_off[0], nbytes // 4) if kind == "f" else bv(tmp_off[0], nbytes // 2))
            tmp_off[0] += nbytes
        return out

    RA = ring(2048, 2, "f")
    RF = ring(2048, 2, "f")
    RD = ring(2048, 2, "f")
    RB = ring(2048, 3, "f")
    RE = ring(2048, 1, "f")
    RH = ring(2048, 2, "f")
    REL = ring(32, 4, "f")
    RKB = ring(1024, 2, "b")
    RV = ring(1024, 5, "b")
    RKV = [ring(2048, 2, "b"), ring(2048, 2, "b")]
    assert tmp_off[0] <= PT_OFF, tmp_off[0]
    RQ = ring(2048, 2, "f")
    RQR = ring(2048, 4, "f")
    RQB = ring(1024, 3, "b")
    RGR = ring(2048, 2, "f")
    RG = ring(2048, 2, "f")
    RSG = ring(2048, 6, "f")
    RSC = ring(1024, 2, "b")
    RSB = ring(2048, 2, "b")
    assert tmp_off[0] <= ARENA_BYTES, tmp_off[0]
    tmp_off[0] = STG_OFF
    RSQ = ring(2048, 2, "f")
    RL = ring(2048, 1, "f")
    RO = ring(2048, 1, "f")
    assert tmp_off[0] <= X_OFF, tmp_off[0]

    Wf = bv(R_OFF, 16 * 1024).rearrange("p (h k c) -> p h k c", h=8, c=128)
    Wv = bv(R_OFF + 32768, 16 * 1024).rearrange("p (h k c) -> p h k c", h=8, c=128)
    wh_sem = [P.dma_sem("wh%d" % i) for i in range(NH)]
    mixT = bv(R_OFF, 16 * TM).rearrange("p (k t) -> p k t", t=TM)
    WS = [bv(R_OFF + 34816 + 16384 * i, 16 * 512).rearrange("p (k c) -> p k c", c=512) for i in range(2)]
    WS4 = [bv(R_OFF + 34816 + 16384 * i, 16 * 512).rearrange("p (g k c) -> p g k c", g=4, c=128) for i in range(2)]
    ws_sem = [P.dma_sem("ws0"), P.dma_sem("ws1")]
    Xpre = [bv(X_OFF + 16384 * i, 16 * 512).rearrange("p (k t) -> p k t", t=512) for i in range(2)]
    xp_sem = [P.dma_sem("xp0"), P.dma_sem("xp1")]
    Xm = bv(X_OFF, 16 * TM).rearrange("p (k t) -> p k t", t=TM)
    xm_sem = P.dma_sem("xm")
    w_sem = P.dma_sem("wfv")

    step_ctr = [0]
    NSTAGE = 7

    def make_step(h, n, xk, wf, wv, wq=None, wg=None, t0=0, pre=None):
        st = dict(h=h, n=n, xk=xk, wf=wf, wv=wv, wq=wq, wg=wg, t0=t0, i=step_ctr[0],
                  full=wq is not None, pre=pre)
        step_ctr[0] += 1
        return st

    def c3(ap, n):
        return ap[:, 0:n].rearrange("p (c j) -> p c j", j=64)

    proj_state = dict(banks=[0, 1], i=0)

    def pbank():
        bk = banks[proj_state["banks"][proj_state["i"] % len(proj_state["banks"])]]
        proj_state["i"] += 1
        return bk

    def proj(st, w, n):
        bk = pbank()
        for k in range(NK):
            mm(bk[:, 0:n], w(k), st["xk"](k), start=(k == 0), stop=(k == NK - 1))
        return bk

    def R(ringbuf, st):
        return ringbuf[st["i"] % len(ringbuf)]

    def stage0(st):
        h, n, full = st["h"], st["n"], st["full"]
        if st["pre"] is not None:
            st["pre"]()
        tA, tF = R(RA, st), R(RF, st)
        lb_h = lbt[:, h:h + 1]
        oml_h = lbt[:, 8 + h:9 + h]
        Pa = proj(st, st["wf"], n)
        act(tA[:, 0:n], Pa[:, 0:n], AF.Exp, scale=-1.0)
        Pb = proj(st, st["wv"], n)
        act(R(RV, st)[:, 0:n], Pb[:, 0:n], AF.Copy)
        act(tA[:, 0:n], tA[:, 0:n], AF.Ln, bias=1.0)
        act(tA[:, 0:n], tA[:, 0:n], AF.Exp, scale=-1.0)
        act(tF[:, 0:n], tA[:, 0:n], AF.Ln, bias=lb_h, scale=oml_h)
        if full:
            tG = R(RG, st)
            Pc = proj(st, st["wq"], n)
            cp("dve", R(RQR, st)[:, 0:n], Pc[:, 0:n])
            Pd = proj(st, st["wg"], n)
            act(tG[:, 0:n], Pd[:, 0:n], AF.Exp, scale=-1.0)
            cp("dve", R(RGR, st)[:, 0:n], Pd[:, 0:n])
            act(tG[:, 0:n], tG[:, 0:n], AF.Ln, bias=1.0)
            act(tG[:, 0:n], tG[:, 0:n], AF.Exp, scale=-1.0)

    def stage1(st):
        h, n, full = st["h"], st["n"], st["full"]
        tA, tF, tD, tB = R(RA, st), R(RF, st), R(RD, st), R(RB, st)
        P.op("dve", lambda e: e.tensor_tensor_scan(tD[:, 0:n], rmask[:, 0:n], tF[:, 0:n], 0.0, ALU.mult, ALU.add),
             reads=[rmask[:, 0:n], tF[:, 0:n]], writes=[tD[:, 0:n]])
        ts("pool", tB[:, 0:n], tA[:, 0:n], lbt[:, 16 + h:17 + h], lbt[:, 8 + h:9 + h], ALU.mult, ALU.add)
        if full:
            stt(R(RSG, st)[:, 0:n], R(RGR, st)[:, 0:n], vec[:, V_GN + h:V_GN + h + 1], R(RG, st)[:, 0:n],
                ALU.mult, ALU.mult)

    def stage2(st):
        n, full = st["n"], st["full"]
        nch = n // 64
        tD, tE, tH = R(RD, st), R(RE, st), R(RH, st)
        bl = tD[:, 63:n:64]
        tt("pool", c3(tE, n), bl.unsqueeze(2).broadcast_to([128, nch, 64]), c3(tD, n), ALU.subtract)
        act(R(REL, st)[:, 0:nch], bl, AF.Exp)
        act(tH[:, 0:n], tE[:, 0:n], AF.Exp)
        if full:
            act(R(RQ, st)[:, 0:n], tE[:, 0:n], AF.Exp, scale=-1.0)

    def stage3(st):
        n, full = st["n"], st["full"]
        tt("pool", R(RKB, st)[:, 0:n], R(RB, st)[:, 0:n], R(RH, st)[:, 0:n], ALU.mult)
        if full:
            tt("dve", R(RQB, st)[:, 0:n], R(RQR, st)[:, 0:n], R(RQ, st)[:, 0:n], ALU.mult)

    def stage4a(st):
        n = st["n"]
        nch = n // 64
        kbT, vT = R(RKB, st), R(RV, st)
        groups = [(0, min(4, nch))] + ([(4, nch)] if nch > 4 else [])
        for gi, (c0, c1) in enumerate(groups):
            bankT = banks[2 + gi].bitcast(BF16)
            for c in range(c0, c1):
                j = c - c0
                tr(bankT[0:64, j * 128:(j + 1) * 128], kbT[:, 64 * c:64 * c + 64])
                tr(bankT[0:64, 512 + j * 128:512 + (j + 1) * 128], vT[:, 64 * c:64 * c + 64])
        for gi, (c0, c1) in enumerate(groups):
            ng = c1 - c0
            bankT = banks[2 + gi].bitcast(BF16)
            src = bankT[0:64, :].rearrange("p (a b) -> p a b", a=2)[:, :, 0:ng * 128]
            dst = R(RKV[gi], st)[0:64, :].rearrange("p (a b) -> p a b", a=2)[:, :, 0:ng * 128]
            cp("act", dst, src)

    def stage4b(st):
        h, n, full = st["h"], st["n"], st["full"]
        nch = n // 64
        kbT, el = R(RKB, st), R(REL, st)
        Sbf = R(RSB, st)
        Sh = Sst[:, h * 128:(h + 1) * 128]
        groups = [(0, min(4, nch))] + ([(4, nch)] if nch > 4 else [])
        for gi, (c0, c1) in enumerate(groups):
            bankI = banks[4 + gi]
            kv = R(RKV[gi], st)
            for c in range(c0, c1):
                j = c - c0
                mm(bankI[:, j * 128:(j + 1) * 128], kv[0:64, j * 128:(j + 1) * 128],
                   kv[0:64, 512 + j * 128:512 + (j + 1) * 128])
        if full:
            qb = R(RQB, st)
            bankSc = banks[6]
            for c in range(nch):
                mm(bankSc[0:64, 64 * c:64 * c + 64], kbT[:, 64 * c:64 * c + 64], qb[:, 64 * c:64 * c + 64])
        if full:
            tt("dve", R(RSC, st)[0:64, 0:n], bankSc[0:64, 0:n], maskT[0:64, 0:n], ALU.mult)
        for gi, (c0, c1) in enumerate(groups):
            bankI = banks[4 + gi]
            for c in range(c0, c1):
                j = c - c0
                e_c = el[:, c:c + 1]
                if full:
                    ts("dve", Sbf[:, c * 128:(c + 1) * 128], Sh, e_c, None, ALU.mult)
                stt(Sh, Sh, e_c, bankI[:, j * 128:(j + 1) * 128], ALU.mult, ALU.add)

    def stage5a(st):
        n, full = st["n"], st["full"]
        if not full:
            return
        nch = n // 64
        qb, scm, Sbf = R(RQB, st), R(RSC, st), R(RSB, st)
        bankO = banks[7]
        for c in range(nch):
            j = c % 4
            mm(bankO[:, 64 * c:64 * c + 64], Sbf[:, c * 128:(c + 1) * 128], qb[:, 64 * c:64 * c + 64],
               start=True, stop=False)
            mm(bankO[:, 64 * c:64 * c + 64], R(RKV[c // 4], st)[0:64, 512 + j * 128:512 + (j + 1) * 128],
               scm[0:64, 64 * c:64 * c + 64], start=False, stop=True)
        act(R(RSQ, st)[:, 0:n], bankO[:, 0:n], AF.Square)

    def stage5b(st):
        h, n, full, t0 = st["h"], st["n"], st["full"], st["t0"]
        if not full:
            return
        sq, tL, tO = R(RSQ, st), R(RL, st), R(RO, st)
        bankO = banks[7]
        bankM = pbank()
        mm(bankM[:, 0:n], onesd[:], sq[:, 0:n])
        act(tL[:, 0:n], bankM[:, 0:n], AF.Ln, bias=RMS_EPS)
        act(tL[:, 0:n], tL[:, 0:n], AF.Exp, scale=-0.5)
        tt("dve", tO[:, 0:n], bankO[:, 0:n], tL[:, 0:n], ALU.mult)
        tt("dve", mixT[:, 8 + h, t0:t0 + n], tO[:, 0:n], R(RSG, st)[:, 0:n], ALU.mult)

    TICK = [(stage4a, 4), (stage5b, 6), (stage3, 3), (stage2, 2), (stage1, 1), (stage0, 0), (stage4b, 4), (stage5a, 5)]

    def run_steps(steps):
        ns = len(steps)
        for t in range(ns + NSTAGE - 1):
            for fn, j in TICK:
                i = t - j
                if 0 <= i < ns:
                    fn(steps[i])

    def load_pre_tile(ti):
        t0, t1 = PRE_TILES[ti]
        n = t1 - t0
        fill([(Xpre[ti % 2][:, :, 0:n], kview(xT_h[:, t0:t1]))], xp_sem[ti % 2])

    def finish_early():
        osem = P.dma_sem("out")
        P.dma("pool", outT_h[0:128, :], fv(X_OFF, 1024), osem, reads=[fv(X_OFF, 1024)], final=True)
        if debug:
            dsem = P.dma_sem("dbg")
            P.dma("pool", dbg_h, bv(R_OFF, 16 * TM), dsem, reads=[bv(R_OFF, 16 * TM)], final=True)
        P.emit()
        return nc, P

    if stop_after == "consts":
        return finish_early()
    steps = []
    load_pre_tile(0)
    for h in range(NH):
        fill([(Wf[:, h, :, :], kview(w_in_h[:, 2048 + h * 128:2048 + (h + 1) * 128])),
              (Wv[:, h, :, :], kview(w_in_h[:, 3072 + h * 128:3072 + (h + 1) * 128]))], wh_sem[h])
    pre_tiles = PRE_TILES[:1] if stop_after == "A1" else PRE_TILES
    for ti, (t0, t1) in enumerate(pre_tiles):
        n = t1 - t0
        xb = Xpre[ti % 2]
        for h in range(NH):
            pre = None
            if h == 0 and ti + 1 < len(pre_tiles):
                pre = (lambda ti=ti: load_pre_tile(ti + 1))
            steps.append(make_step(
                h, n,
                xk=(lambda k, xb=xb, n=n: xb[:, k, 0:n]),
                wf=(lambda k, h=h: Wf[:, h, k, :]),
                wv=(lambda k, h=h: Wv[:, h, k, :]), pre=pre))
    if stop_after == "A1load":
        return finish_early()
    proj_state["banks"] = [0, 1, 6, 7]
    stepsA = steps

    def b_prologue():
        proj_state["banks"] = [0, 1]
        fill([(Xm[:, 0:8, :], kview(xT_h[0:1024, TP:TP + TM])), (Xm[:, 8:16, :], kview(xT_h[1024:2048, TP:TP + TM]))], xm_sem)
        for half in range(2):
            fill([(WS[half][:, :, :], kview(w_in_h[:, half * 512:(half + 1) * 512]))], ws_sem[half])
        o = PT_OFF
        PADW = 16 + TM
        U0s = [fv(o, PADW), fv(o + 4 * PADW, PADW)]; o += 8 * PADW
        A1s = [fv(o, PADW), fv(o + 4 * PADW, PADW)]; o += 8 * PADW
        A2s = [fv(o, PADW), fv(o + 4 * PADW, PADW)]; o += 8 * PADW
        PB = [bv(o, TM), bv(o + 2 * TM, TM)]; o += 4 * TM
        PW = bv(o, 8 * 256).rearrange("p (r d) -> p r d", d=256); o += 4096
        t16 = fv(o, 16); o += 64
        assert o <= ARENA_BYTES, o
        pw_sem = P.dma_sem("pw")
        fill([(PW[:, :, :], pool_w_h.rearrange("(r p) d -> p r d", p=128))], pw_sem)
        for buf in U0s + A1s + A2s:
            P.op("pool", (lambda e, buf=buf: e.memset(buf[:, 0:16], 0.0)), writes=[buf[:, 0:16]])

        for cc in range(8):
            g = cc // 2
            w = WINDOWS[g]
            ws = WS[cc // 4]
            col = (cc % 4) * 128
            U0, A1, A2 = U0s[cc % 2], A1s[cc % 2], A2s[cc % 2]
            for (t0, t1) in MAIN_TILES:
                n = t1 - t0
                bk = next_proj()
                for k in range(NK):
                    mm(bk[:, 0:n], ws[:, k, col:col + 128], Xm[:, k, t0:t1], start=(k == 0), stop=(k == NK - 1))
                act(U0[:, 16 + t0:16 + t1], bk[:, 0:n], AF.Copy)
            cur = U0
            seq = [A1, A2, A1, A2]
            sh = 1
            i = 0
            while sh < w:
                nxt = seq[i]
                tt("dve", nxt[:, 16:16 + TM], cur[:, 16:16 + TM], cur[:, 16 - sh:16 - sh + TM], ALU.add)
                cur = nxt
                sh *= 2
                i += 1
            pb = PB[cc % 2]
            stt(pb[:, 0:TM], cur[:, 16:16 + TM], 1.0 / w, U0[:, 16:16 + TM], ALU.mult, ALU.subtract)
            tt("dve", t16[:, 0:16], cur[:, 16 + 64:16 + 80], invc[:, g * 16:(g + 1) * 16], ALU.mult)
            tt("dve", pb[:, 64:80], t16[:, 0:16], U0[:, 16 + 64:16 + 80], ALU.subtract)
            if cc % 2 == 1:
                for dc in range(2):
                    oc = g * 2 + dc
                    for (t0, t1) in MAIN_TILES:
                        n = t1 - t0
                        bk = next_proj()
                        for ci in range(2):
                            mm(bk[:, 0:n], PW[:, g * 2 + ci, dc * 128:(dc + 1) * 128], PB[ci][:, t0:t1],
                               start=(ci == 0), stop=(ci == 1))
                        ts("dve", mixT[:, oc, t0:t1], bk[:, 0:n], vec[:, V_PB + oc:V_PB + oc + 1],
                           vec[:, V_PS + oc:V_PS + oc + 1], ALU.add, ALU.mult)

        load_head(0)

    if stop_after == "poolmix":
        return finish_early()

    def load_head(h):
        slot = WS4[h % 2]
        fill([(slot[:, gi, :, :], kview(w_in_h[:, base + h * 128:base + (h + 1) * 128]))
              for gi, base in enumerate((1024, 2048, 3072, 4096))], ws_sem[h % 2])

    if stop_after == "B1load":
        load_head(0)
        mm(banks[0][:, 0:512], WS4[0][:, 0, 0, :], Xm[:, 0, 0:512])
        return finish_early()
    steps = []
    nheads = 1 if stop_after in ("B1", "B1a") else NH
    for h in range(nheads):
        slot = WS4[h % 2]
        for tix, (t0, t1) in enumerate(MAIN_TILES[:1] if stop_after == "B1a" else MAIN_TILES):
            n = t1 - t0
            pre = None
            if tix == 0 and h + 1 < nheads:
                pre = (lambda h=h: load_head(h + 1))
            if tix == 0 and h == 0:
                pre = (lambda: (b_prologue(), load_head(1)))
            steps.append(make_step(
                h, n,
                xk=(lambda k, t0=t0, t1=t1: Xm[:, k, t0:t1]),
                wq=(lambda k, slot=slot: slot[:, 0, k, :]),
                wf=(lambda k, slot=slot: slot[:, 1, k, :]),
                wv=(lambda k, slot=slot: slot[:, 2, k, :]),
                wg=(lambda k, slot=slot: slot[:, 3, k, :]),
                t0=t0, pre=pre))
    run_steps(stepsA + steps)

    if stop_after in ("mixer", "B1", "B1a"):
        return finish_early()

    Z1_OFFS = [X_OFF + 4352 * d for d in range(8)] + [TMP_OFF + 4352 * d for d in range(8)]
    z1 = [fv(Z1_OFFS[d], TM) for d in range(16)]
    o = TMP_OFF + 8 * 4352
    XR = [fv(o, TM), fv(o + 4352, TM)]; o += 8704
    XF = [fv(o, TM), fv(o + 4352, TM)]; o += 8704
    sqx = [fv(o, 512), fv(o + 2048, 512)]; o += 4096
    assert o <= ARENA_BYTES, o
    xr_sem = [P.dma_sem("xr0"), P.dma_sem("xr1")]
    mu = fv(R_OFF + 34816, TM)
    rs = fv(R_OFF + 34816 + 4352, TM)
    msq = fv(R_OFF + 34816 + 8704, TM)
    x1T = bv(R_OFF, 16 * T1).rearrange("p (k t) -> p k t", t=T1)

    def load_wout(g):
        fill([(WS[g % 2][:, :, :], kview(w_out_h[:, g * 512:(g + 1) * 512]))], ws_sem[g % 2])

    acc1 = fv(o, TM); o += 4 * TM
    acc2 = fv(o, TM); o += 4 * TM
    assert o <= ARENA_BYTES, o
    sq_i = [0]

    def stat_acc(zap, a1, a2, first, e1="pool", e2="pool"):
        sb = sqx[sq_i[0] % 2]
        sq_i[0] += 1
        n_ = zap.shape[1]
        if first:
            cp(e1, a1, zap)
            act(a2, zap, AF.Square)
        else:
            act(sb[:, 0:n_], zap, AF.Square)
            tt(e1, a1, a1, zap, ALU.add)
            tt(e2, a2, a2, sb[:, 0:n_], ALU.add)

    def ln_finish(a1, a2, tiles, mu_, rs_, msq_, eps):
        for (t0, t1) in tiles:
            n = t1 - t0
            b0 = next_proj()
            b1 = next_proj()
            mm(b0[:, 0:n], onesf[:], a1[:, t0:t1])
            mm(b1[:, 0:n], onesf[:], a2[:, t0:t1])
            ts("dve", mu_[:, t0:t1], b0[:, 0:n], 1.0 / D, None, ALU.mult)
            tt("dve", msq_[:, t0:t1], mu_[:, t0:t1], mu_[:, t0:t1], ALU.mult)
            stt(msq_[:, t0:t1], b1[:, 0:n], 1.0 / D, msq_[:, t0:t1], ALU.mult, ALU.subtract)
            act(rs_[:, t0:t1], msq_[:, t0:t1], AF.Ln, bias=eps)
            act(rs_[:, t0:t1], rs_[:, t0:t1], AF.Exp, scale=-0.5)

    load_wout(0)
    for g in range(4):
        if g + 1 < 4:
            load_wout(g + 1)
        for dc in range(4):
            d = 4 * g + dc
            xr = XR[d % 2]
            P.dma("sp", xr[:, 0:TM], xT_h[d * 128:(d + 1) * 128, TP:TP + TM], xr_sem[d % 2], writes=[xr[:, 0:TM]])
            for (t0, t1) in MAIN_TILES:
                n = t1 - t0
                bk = next_proj()
                for k in range(NK):
                    mm(bk[:, 0:n], WS[g % 2][:, k, dc * 128:(dc + 1) * 128], mixT[:, k, t0:t1],
                       start=(k == 0), stop=(k == NK - 1))
                stt(z1[d][:, t0:t1], xr[:, t0:t1], ALPHA, bk[:, 0:n], ALU.mult, ALU.add)
                stat_acc(z1[d][:, t0:t1], acc1[:, t0:t1], acc2[:, t0:t1], d == 0, e1="dve", e2="dve")

    ln_finish(acc1, acc2, MAIN_TILES, mu, rs, msq, LN_EPS)
    x1_sem = [P.dma_sem("x1s0"), P.dma_sem("x1s1")]
    WD_OFF = ARENA_BYTES - 16384
    assert o <= WD_OFF
    WD = [bv(WD_OFF + 4096 * i, 16 * 128).rearrange("p (k c) -> p k c", c=128) for i in range(4)]
    wd_sem = [P.dma_sem("wd0"), P.dma_sem("wd1")]

    def load_wup(p):
        fill([(WD[(p % 2) * 2 + half][:, :, :], kview(w_up_h[:, half * DFF + p * 128:half * DFF + (p + 1) * 128]))
              for half in range(2)], wd_sem[p % 2])

    load_wup(0)
    load_wup(1)
    nb1 = msq
    stt(nb1[:, 0:TM], mu[:, 0:TM], -1.0, rs[:, 0:TM], ALU.mult, ALU.mult)
    for d in range(16):
        tt("dve", z1[d][:, 0:TM], z1[d][:, 0:TM], rs[:, 0:TM], ALU.mult)
        tt("dve", z1[d][:, 0:TM], z1[d][:, 0:TM], nb1[:, 0:TM], ALU.add)
        act(x1T[:, d, 0:2], z1[d][:, 62:64], AF.Identity, bias=gfb[:, 16 + d:17 + d], scale=gfb[:, d:d + 1])
        act(x1T[:, d, 2:T1], z1[d][:, 64:TM], AF.Identity, bias=vec[:, V_B1 + d:V_B1 + d + 1],
            scale=vec[:, V_G1 + d:V_G1 + d + 1])
    XF4 = [XF[0], XF[1], XR[0], XR[1]]
    x1_sem4 = x1_sem + [P.dma_sem("x1s2"), P.dma_sem("x1s3")]
    for d in range(16):
        xf = XF4[d % 4]
        ts("dve", xf[:, 0:TO], z1[d][:, 64:TM], vec[:, V_G1 + d:V_G1 + d + 1], vec[:, V_B1 + d:V_B1 + d + 1],
           ALU.mult, ALU.add)
        P.dma("sp", x1s_h[d * 128:(d + 1) * 128, :], xf[:, 0:TO], x1_sem4[d % 4],
              reads=[xf[:, 0:TO]], writes=[("x1s", d)])

    if stop_after == "ln1":
        if debug:
            dsem = P.dma_sem("dbg")
            P.dma("pool", dbg_h[:, 0:16 * T1], bv(R_OFF, 16 * T1), dsem, reads=[bv(R_OFF, 16 * T1)], final=True)
        osem = P.dma_sem("out")
        P.dma("pool", outT_h[0:128, :], fv(X_OFF, 1024), osem, reads=[fv(X_OFF, 1024)], final=True)
        P.emit()
        return nc, P

    HT_OFF = R_OFF + 2 * 16 * T1
    assert HT_OFF + NFF * TO * 2 <= ARENA_BYTES - 4112, HT_OFF
    hT = bv(HT_OFF, NFF * TO).rearrange("p (k t) -> p k t", t=TO)
    o = X_OFF
    UR = [fv(o, 1028), fv(o + 4112, 1028)]; o += 8224
    ACC = [fv(o, 1028), fv(o + 4112, 1028)]; o += 8224
    assert o <= R_OFF, o
    SG = fv(HT_OFF + NFF * TO * 2, 1028)
    assert HT_OFF + NFF * TO * 2 + 4112 <= ARENA_BYTES

    for p in range(NFF):
        if 2 <= p + 1 < NFF:
            load_wup(p + 1)
        for half in range(2):
            wd = WD[(p % 2) * 2 + half]
            cidx = half * NFF + p
            bks = [next_proj() for _ in FF_TILES]
            for k in range(NK):
                for ti, (t0, t1) in enumerate(FF_TILES):
                    mm(bks[ti][:, 0:t1 - t0], wd[:, k, :], x1T[:, k, t0:t1], start=(k == 0), stop=(k == NK - 1))
            ur = UR[half]
            for ti, (t0, t1) in enumerate(FF_TILES):
                act(ur[:, t0:t1], bks[ti][:, 0:t1 - t0], AF.Copy)
            acc = ACC[half]
            cw = lambda j: vec[:, V_CW + 88 * j + cidx:V_CW + 88 * j + cidx + 1]
            ts("dve", acc[:, 0:TO], ur[:, 2:2 + TO], cw(2), vec[:, V_CB + cidx:V_CB + cidx + 1], ALU.mult, ALU.add)
            stt(acc[:, 0:TO], ur[:, 1:1 + TO], cw(1), acc[:, 0:TO], ALU.mult, ALU.add)
            stt(acc[:, 0:TO], ur[:, 0:TO], cw(0), acc[:, 0:TO], ALU.mult, ALU.add)
            if half == 0:
                act(SG[:, 0:TO], acc[:, 0:TO], AF.Silu)
            else:
                tt("dve", hT[:, p, :], SG[:, 0:TO], acc[:, 0:TO], ALU.mult)

    NWR = 8
    WR_OFF = X_OFF + 4096 * 14
    Z2_OFFS = [X_OFF + 4096 * d for d in range(14)] + [HT_OFF + 4096 * d for d in range(2)]
    z2 = [fv(Z2_OFFS[d], TO) for d in range(16)]
    WR = [bv(WR_OFF + 1024 * i, 2 * 256).rearrange("p (a c) -> p a c", c=256) for i in range(NWR)]
    assert WR_OFF + 1024 * NWR <= HT_OFF
    XS = [fv(STG_OFF, 1024), fv(STG_OFF + 4096, 1024)]
    wr_sem = [P.dma_sem("wr%d" % i) for i in range(NWR)]
    xs_sem = [P.dma_sem("xs0"), P.dma_sem("xs1")]
    OF = [fv(HT_OFF + 8192, TO), fv(HT_OFF + 8192 + 4096, TO), fv(HT_OFF + 28672, TO), fv(HT_OFF + 32768, TO)]
    of_sem = [P.dma_sem("of%d" % i) for i in range(4)]
    sqx = [fv(HT_OFF + NFF * TO * 2, 512), fv(HT_OFF + NFF * TO * 2 + 2048, 512)]
    acc1 = fv(HT_OFF + NFF * TO * 2 + 4096, TO)
    acc2 = fv(HT_OFF + NFF * TO * 2 + 8192, TO)
    assert HT_OFF + NFF * TO * 2 + 12288 <= ARENA_BYTES
    mu2 = fv(HT_OFF + 16384, TO)
    rs2 = fv(HT_OFF + 16384 + 4096, TO)
    msq2 = fv(HT_OFF + 16384 + 8192, TO)
    E_TILES = [(0, 512), (512, 1024)]
    wr_i = 0
    for g in range(8):
        accb = [banks[(g % 2) * 4 + i] for i in range(4)]
        for k2 in range(NFF // 2):
            slot = WR[wr_i % NWR]
            src = w_down_h[k2 * 256:(k2 + 1) * 256, g * 256:(g + 1) * 256].rearrange("(a p) c -> p a c", p=128)
            fill([(slot[:, :, :], src)], wr_sem[wr_i % NWR])
            wr_i += 1
            for a in range(2):
                k = 2 * k2 + a
                for dc in range(2):
                    for ti, (t0, t1) in enumerate(E_TILES):
                        mm(accb[dc * 2 + ti][:, 0:512], slot[:, a, dc * 128:(dc + 1) * 128], hT[:, k, t0:t1],
                           start=(k == 0), stop=(k == NFF - 1))
        for dc in range(2):
            d = 2 * g + dc
            xs = XS[d % 2]
            P.dma("sp", xs[:, 0:TO], x1s_h[d * 128:(d + 1) * 128, :], xs_sem[d % 2],
                  reads=[("x1s", d)], writes=[xs[:, 0:TO]])
            for ti, (t0, t1) in enumerate(E_TILES):
                stt(z2[d][:, t0:t1], xs[:, t0:t1], ALPHA, accb[dc * 2 + ti][:, 0:512], ALU.mult, ALU.add)
        for dc in range(2):
            d = 2 * g + dc
            for ti, (t0, t1) in enumerate(E_TILES):
                stat_acc(z2[d][:, t0:t1], acc1[:, t0:t1], acc2[:, t0:t1], d == 0, e1="dve", e2="dve")
    ln_finish(acc1, acc2, E_TILES, mu2, rs2, msq2, LN_EPS)
    nb2 = msq2
    stt(nb2[:, 0:TO], mu2[:, 0:TO], -1.0, rs2[:, 0:TO], ALU.mult, ALU.mult)
    for d in range(16):
        tt("dve", z2[d][:, 0:TO], z2[d][:, 0:TO], rs2[:, 0:TO], ALU.mult)
        tt("dve", z2[d][:, 0:TO], z2[d][:, 0:TO], nb2[:, 0:TO], ALU.add)
        of = OF[d % 4]
        act(of[:, 0:TO], z2[d][:, 0:TO], AF.Identity, bias=vec[:, V_B2 + d:V_B2 + d + 1],
            scale=vec[:, V_G2 + d:V_G2 + d + 1])
        P.dma("sp", outT_h[d * 128:(d + 1) * 128, :], of[:, 0:TO], of_sem[d % 4], reads=[of[:, 0:TO]], final=True)
    P.emit()
    return nc, P


def _cols(v, n):
    return np.ascontiguousarray(np.asarray(v, np.float32).reshape(n, 128).T)


def prep_inputs(x, w_in, pool_w, pool_b, pool_scale, hgrn_lb_logits, hgrn_g_norm, w_out,
                ln1_g, ln1_b, w_up, conv_w, conv_b, w_down, ln2_g, ln2_b):
    f32 = lambda a: np.ascontiguousarray(np.asarray(a, np.float32))
    x = f32(x)
    vec = np.zeros((128, NV), np.float32)
    vec[:, V_PB:V_PB + 8] = _cols(np.asarray(pool_b)[0].reshape(-1), 8)
    vec[:, V_PS:V_PS + 8] = _cols(np.asarray(pool_scale)[0], 8)
    vec[:, V_L0:V_L0 + 8] = _cols(np.asarray(hgrn_lb_logits)[0], 8)
    vec[:, V_L1:V_L1 + 8] = _cols(np.asarray(hgrn_lb_logits)[1], 8)
    vec[:, V_GN:V_GN + 8] = _cols(np.asarray(hgrn_g_norm)[0], 8)
    cw = np.asarray(conv_w, np.float32)[0]
    for j in range(3):
        vec[:, V_CW + 88 * j:V_CW + 88 * (j + 1)] = _cols(cw[j], 88)
    vec[:, V_CB:V_CB + 88] = _cols(np.asarray(conv_b)[0], 88)
    vec[:, V_G1:V_G1 + 16] = _cols(np.asarray(ln1_g)[0], 16)
    vec[:, V_B1:V_B1 + 16] = _cols(np.asarray(ln1_b)[0], 16)
    vec[:, V_G2:V_G2 + 16] = _cols(np.asarray(ln2_g)[0], 16)
    vec[:, V_B2:V_B2 + 16] = _cols(np.asarray(ln2_b)[0], 16)
    ident = np.eye(128, dtype=np.float32).astype(ml_dtypes.bfloat16)
    m = np.triu(np.ones((64, 64), np.float32))
    maskT = np.ascontiguousarray(np.tile(m, (1, 8)))
    rmask = np.ones((1, 512), np.float32)
    rmask[0, ::64] = 0.0
    shared = dict(
        w_in=f32(w_in)[0], pool_w=f32(pool_w)[0].reshape(1024, 256), w_out=f32(w_out)[0],
        w_up=f32(w_up)[0], w_down=f32(w_down)[0], vec=vec, ident=ident, maskT=maskT, rmask=rmask)
    in_maps = []
    for c in range(8):
        b, j = divmod(c, 4)
        npre = 1024 * (j + 1)
        xcat = np.zeros((SEQ, D), np.float32)
        xcat[SEQ - npre:] = x[b, :npre]
        invc = np.zeros((1, 64), np.float32)
        for g, w in enumerate(WINDOWS):
            for i in range(16):
                invc[0, g * 16 + i] = 1.0 / min(1024 * j + i + 1, w)
        flag = np.full((1, 1), 0.0 if j == 0 else 1.0, np.float32)
        m_ = dict(shared)
        m_.update(xT=np.ascontiguousarray(xcat.T), invc=invc, flag=flag)
        in_maps.append(m_)
    return in_maps


def kernel(**inputs):
    in_maps = prep_inputs(**inputs)
    nc, _ = build_program()
    res = run_bass_kernel_spmd(nc, in_maps, core_ids=list(range(8)))
    out = np.zeros((2, SEQ, D), np.float32)
    for c in range(8):
        b, j = divmod(c, 4)
        out[b, 1024 * j:1024 * (j + 1), :] = res.results[c]["outT"].T
    return out
```

```python
import numpy as np
import ml_dtypes
import concourse.bass as bass
import concourse.mybir as mybir
from concourse.bass_utils import run_bass_kernel_spmd

F32 = mybir.dt.float32
BF16 = mybir.dt.bfloat16
AF = mybir.ActivationFunctionType
ALU = mybir.AluOpType


class _Op:
    __slots__ = ("eng", "fn", "deps", "is_dma", "sem", "val", "signal", "idx", "final", "inc")

    def __init__(self, eng, fn):
        self.eng = eng
        self.fn = fn
        self.deps = []
        self.is_dma = False
        self.sem = None
        self.val = 0
        self.signal = False
        self.idx = 0
        self.final = False
        self.inc = 16


class Prog:
    ENGS = ("pe", "act", "dve", "pool", "sp")

    def __init__(self, nc):
        self.nc = nc
        self.ops = {e: [] for e in self.ENGS}
        self.recs = {}
        self.esem = {e: nc.alloc_semaphore(name="ctr_" + e) for e in ("pe", "act", "dve", "pool")}
        self.dma_sems = {}
        self.dma_cnt = {}
        self.finals = []

    def sbuf(self, name, shape, dtype):
        return self.nc.alloc_sbuf_tensor("sb_" + name, shape, dtype)

    def psum(self, name, shape, dtype):
        return self.nc.alloc_psum_tensor("ps_" + name, shape, dtype)

    def dma_sem(self, name):
        s = self.nc.alloc_semaphore(name="dma_" + name)
        self.dma_cnt[id(s)] = 0
        return s

    @staticmethod
    def _region(a):
        if isinstance(a, tuple):
            return (a, 0, 1)
        cls = type(a.tensor).__name__
        if cls == "PSumTensorHandle":
            return (a.tensor.name, 0, 2048, True)
        if cls != "SBTensorHandle":
            return None
        es = mybir.dt.size(a.dtype)
        ap = a.ap
        row = ap[0][0]
        off = a.offset % row if row > 0 else a.offset
        ext = 1
        for st, cnt in ap[1:]:
            ext += (cnt - 1) * abs(st)
        return (a.tensor.name, off * es, (off + ext) * es)

    def _access(self, op, reg, is_write, deps):
        name, lo, hi = reg[0], reg[1], reg[2]
        if len(reg) > 3:
            is_write = True
        recs = self.recs.get(name, [])
        new = []
        for r in recs:
            rlo, rhi, rop, rw = r
            if rhi <= lo or hi <= rlo:
                new.append(r)
                continue
            if (rw or is_write) and rop is not op:
                deps.append(rop)
            if is_write and lo <= rlo and rhi <= hi:
                continue
            if (not is_write) and (not rw) and rlo == lo and rhi == hi and rop.eng == op.eng \
                    and not rop.is_dma and not op.is_dma:
                continue
            new.append(r)
        new.append((lo, hi, op, is_write))
        self.recs[name] = new

    def _track(self, op, reads, writes):
        deps = []
        for a in reads:
            reg = self._region(a)
            if reg is not None:
                self._access(op, reg, False, deps)
        for a in writes:
            reg = self._region(a)
            if reg is not None:
                self._access(op, reg, True, deps)
        seen = set()
        for d in deps:
            if d is op or id(d) in seen:
                continue
            seen.add(id(d))
            op.deps.append(d)

    def op(self, eng, fn, reads=(), writes=()):
        o = _Op(eng, fn)
        self._track(o, reads, writes)
        self.ops[eng].append(o)
        return o

    def dma(self, eng, out, in_, sem, reads=(), writes=(), final=False, **kw):
        o = _Op(eng, lambda e: e.dma_start(out=out, in_=in_, **kw))
        o.is_dma = True
        o.sem = sem
        self.dma_cnt[id(sem)] += 16
        o.val = self.dma_cnt[id(sem)]
        o.final = final
        self._track(o, reads, writes)
        self.ops[eng].append(o)
        if final:
            self.finals.append(o)
        return o

    def emit(self):
        nc = self.nc
        for e in self.ENGS:
            for o in self.ops[e]:
                for d in o.deps:
                    if d.is_dma:
                        continue
                    if d.eng == "pe" and o.eng == "pe" and not o.is_dma:
                        continue
                    d.signal = True
        for e in self.ENGS:
            n = 0
            for o in self.ops[e]:
                if o.is_dma:
                    continue
                if o.signal:
                    n += 1
                    o.idx = n
        self.sig_counts = {e: sum(1 for o in self.ops[e] if o.signal) for e in self.ENGS}
        handles = {"pe": "tensor", "act": "scalar", "dve": "vector", "pool": "gpsimd", "sp": "sync"}

        def run(ename, eng):
            waited = {}
            for o in self.ops[ename]:
                for d in o.deps:
                    if d.is_dma:
                        sem, val = d.sem, d.val
                    else:
                        if d.eng == "pe" and ename == "pe" and not o.is_dma:
                            continue
                        sem, val = self.esem[d.eng], d.idx
                    if waited.get(id(sem), 0) >= val:
                        continue
                    waited[id(sem)] = val
                    eng.wait_ge(sem, val)
                ins = o.fn(eng)
                if o.is_dma:
                    ins.then_inc(o.sem, o.inc)
                elif o.signal:
                    ins.then_inc(self.esem[ename], 1)
            if ename == "sp":
                for o in self.finals:
                    if waited.get(id(o.sem), 0) >= o.val:
                        continue
                    waited[id(o.sem)] = o.val
                    eng.wait_ge(o.sem, o.val)

        with nc.Block() as block:
            for ename in self.ENGS:
                def mk(ename):
                    def f(eng):
                        run(ename, eng)
                    return f
                getattr(block, handles[ename])(mk(ename))


D = 2048
NK = 16
SEQ = 4096
TP = 3008
TM = 1088
TO = 1024
DFF = 5632
NFF = 44
NH = 8
ALPHA = 2.0 ** 0.25
LN_EPS = 1e-5
RMS_EPS = 1e-6
WINDOWS = (2, 4, 8, 16)
PRE_TILES = [(0, 512), (512, 1024), (1024, 1536), (1536, 2048), (2048, 2560), (2560, 3008)]
MAIN_TILES = [(0, 512), (512, 1024), (1024, 1088)]
T1 = 1026
FF_TILES = [(0, 342), (342, 684), (684, 1026)]

V_PB, V_PS, V_L0, V_L1, V_GN = 0, 8, 16, 24, 32
V_CW, V_CB = 40, 304
V_G1, V_B1, V_G2, V_B2 = 392, 408, 424, 440
NV = 456

ARENA_BYTES = 200704
STG_OFF = 0
X_OFF = 8192
R_OFF = 43008
TMP_OFF = 112640
PT_OFF = 157696


def build_program(stop_after=None, debug=False):
    nc = bass.Bass("TRN2", target_bir_lowering=False)
    P = Prog(nc)

    def dram(name, shape, dt=F32, kind="ExternalInput"):
        return nc.dram_tensor(name, shape, dt, kind=kind).ap()

    xT_h = dram("xT", [D, SEQ])
    w_in_h = dram("w_in", [D, 5120])
    pool_w_h = dram("pool_w", [1024, 256])
    w_out_h = dram("w_out", [D, D])
    w_up_h = dram("w_up", [D, 2 * DFF])
    w_down_h = dram("w_down", [DFF, D])
    vec_h = dram("vec", [128, NV])
    ident_h = dram("ident", [128, 128], BF16)
    maskT_h = dram("maskT", [64, 512])
    rmask_h = dram("rmask", [1, 512])
    invc_h = dram("invc", [1, 64])
    flag_h = dram("flag", [1, 1])
    outT_h = dram("outT", [D, TO], kind="ExternalOutput")
    x1s_h = dram("x1s", [D, TO], kind="Internal")
    dbg_h = None
    if debug:
        dbg_h = dram("dbg", [128, 16 * TM], BF16, kind="ExternalOutput")

    vec = P.sbuf("vec", [128, NV], F32)
    ident = P.sbuf("ident", [128, 128], BF16)
    maskT = P.sbuf("maskT", [64, 512], F32)
    rmask = P.sbuf("rmask", [128, 512], F32)
    invc = P.sbuf("invc", [128, 64], F32)
    flag = P.sbuf("flag", [128, 1], F32)
    lbt = P.sbuf("lbt", [128, 24], F32)
    Sst = P.sbuf("Sst", [128, NH * 128], F32)
    onesf = P.sbuf("onesf", [128, 128], F32)
    onesd = P.sbuf("onesd", [128, 128], F32)
    arena = P.sbuf("arena", [128, ARENA_BYTES // 4], F32)
    arena_bf = arena.bitcast(BF16)
    banks = [P.psum("bank%d" % i, [128, 512], F32) for i in range(8)]

    def fv(off, n):
        assert off % 4 == 0 and off + 4 * n <= ARENA_BYTES, (off, n)
        return arena[:, off // 4: off // 4 + n]

    def bv(off, n):
        assert off % 2 == 0 and off + 2 * n <= ARENA_BYTES, (off, n)
        return arena_bf[:, off // 2: off // 2 + n]

    def mm(out, lhsT, rhs, start=True, stop=True):
        P.op("pe", lambda e: e.matmul(out, lhsT, rhs, start=start, stop=stop),
             reads=[lhsT, rhs], writes=[out])

    def tr(out, in_):
        P.op("pe", lambda e: e.transpose(out, in_, ident[:]), reads=[in_, ident[:]], writes=[out])

    def act(out, in_, func, bias=None, scale=1.0, extra_reads=()):
        rd = [in_] + list(extra_reads)
        kw = {}
        if bias is not None:
            kw["bias"] = bias
            if not isinstance(bias, (int, float)):
                rd.append(bias)
        if not isinstance(scale, (int, float)):
            rd.append(scale)
        P.op("act", lambda e: e.activation(out, in_, func, scale=scale, **kw), reads=rd, writes=[out])

    def ts(eng, out, in0, s1, s2, op0, op1=None):
        rd = [in0] + [s for s in (s1, s2) if s is not None and not isinstance(s, (int, float))]
        if op1 is None:
            P.op(eng, lambda e: e.tensor_scalar(out, in0, s1, None, op0), reads=rd, writes=[out])
        else:
            P.op(eng, lambda e: e.tensor_scalar(out, in0, s1, s2, op0, op1), reads=rd, writes=[out])

    def stt(out, in0, scalar, in1, op0, op1):
        rd = [in0, in1] + ([] if isinstance(scalar, (int, float)) else [scalar])
        P.op("dve", lambda e: e.scalar_tensor_tensor(out, in0, scalar, in1, op0, op1), reads=rd, writes=[out])

    def tt(eng, out, in0, in1, op):
        P.op(eng, lambda e: e.tensor_tensor(out, in0, in1, op), reads=[in0, in1], writes=[out])

    def cp(eng, out, in_):
        if eng == "act":
            act(out, in_, AF.Copy)
        else:
            P.op(eng, lambda e: e.tensor_copy(out, in_), reads=[in_], writes=[out])

    def fill(pairs, sem, queue="pool"):
        ops_ = [P.dma(queue, dst, src, sem, writes=[dst]) for dst, src in pairs]
        tot = P.dma_cnt[id(sem)]
        for o_ in ops_:
            o_.val = tot

    def kview(h_ap):
        return h_ap.rearrange("(k p) c -> p k c", p=128)

    csem = P.dma_sem("const")
    P.dma("sp", vec[:], vec_h, csem, writes=[vec[:]])
    P.dma("sp", ident[:], ident_h, csem, writes=[ident[:]])
    P.dma("sp", maskT[:], maskT_h, csem, writes=[maskT[:]])
    P.dma("sp", rmask[:], rmask_h.partition_broadcast(128), csem, writes=[rmask[:]])
    P.dma("sp", invc[:], invc_h.partition_broadcast(128), csem, writes=[invc[:]])
    P.dma("sp", flag[:], flag_h.partition_broadcast(128), csem, writes=[flag[:]])
    for o_ in P.ops["sp"]:
        if o_.is_dma and o_.sem is csem:
            o_.val = P.dma_cnt[id(csem)]
    P.op("pool", lambda e: e.memset(onesf[:], 1.0), writes=[onesf[:]])
    P.op("pool", lambda e: e.memset(onesd[:], 1.0 / 128.0), writes=[onesd[:]])
    P.op("pool", lambda e: e.memset(Sst[:], 0.0), writes=[Sst[:]])
    tt("dve", lbt[:, 0:8], vec[:, V_L1:V_L1 + 8], vec[:, V_L0:V_L0 + 8], ALU.subtract)
    act(lbt[:, 0:8], lbt[:, 0:8], AF.Exp)
    ts("dve", lbt[:, 0:8], lbt[:, 0:8], 1.0, None, ALU.add)
    P.op("dve", lambda e: e.reciprocal(lbt[:, 0:8], lbt[:, 0:8]), reads=[lbt[:, 0:8]], writes=[lbt[:, 0:8]])
    ts("dve", lbt[:, 8:16], lbt[:, 0:8], -1.0, 1.0, ALU.mult, ALU.add)
    ts("dve", lbt[:, 16:24], lbt[:, 0:8], -1.0, None, ALU.add)
    gfb = P.sbuf("gfb", [128, 32], F32)
    ts("dve", gfb[:, 0:16], vec[:, V_G1:V_G1 + 16], flag[:, 0:1], None, ALU.mult)
    ts("dve", gfb[:, 16:32], vec[:, V_B1:V_B1 + 16], flag[:, 0:1], None, ALU.mult)

    pb_i = [0]

    def next_proj():
        b = banks[pb_i[0] % 8]
        pb_i[0] += 1
        return b

    tmp_off = [TMP_OFF]

    def ring(nbytes, copies, kind):
        out = []
        for _ in range(copies):
            out.append(fv(tmp_off[0], nbytes // 4) if kind == "f" else bv(tmp_off[0], nbytes // 2))
            tmp_off[0] += nbytes
        return out

    RA = ring(2048, 2, "f")
    RF = ring(2048, 2, "f")
    RD = ring(2048, 2, "f")
    RB = ring(2048, 3, "f")
    RE = ring(2048, 1, "f")
    RH = ring(2048, 2, "f")
    REL = ring(32, 4, "f")
    RKB = ring(1024, 2, "b")
    RV = ring(1024, 5, "b")
    RKV = [ring(2048, 2, "b"), ring(2048, 2, "b")]
    assert tmp_off[0] <= PT_OFF, tmp_off[0]
    RQ = ring(2048, 2, "f")
    RQR = ring(2048, 4, "f")
    RQB = ring(1024, 3, "b")
    RGR = ring(2048, 2, "f")
    RG = ring(2048, 2, "f")
    RSG = ring(2048, 6, "f")
    RSC = ring(1024, 2, "b")
    RSB = ring(2048, 2, "b")
    assert tmp_off[0] <= ARENA_BYTES, tmp_off[0]
    tmp_off[0] = STG_OFF
    RSQ = ring(2048, 2, "f")
    RL = ring(2048, 1, "f")
    RO = ring(2048, 1, "f")
    assert tmp_off[0] <= X_OFF, tmp_off[0]

    Wf = bv(R_OFF, 16 * 1024).rearrange("p (h k c) -> p h k c", h=8, c=128)
    Wv = bv(R_OFF + 32768, 16 * 1024).rearrange("p (h k c) -> p h k c", h=8, c=128)
    wh_sem = [P.dma_sem("wh%d" % i) for i in range(NH)]
    mixT = bv(R_OFF, 16 * TM).rearrange("p (k t) -> p k t", t=TM)
    WS = [bv(R_OFF + 34816 + 16384 * i, 16 * 512).rearrange("p (k c) -> p k c", c=512) for i in range(2)]
    WS4 = [bv(R_OFF + 34816 + 16384 * i, 16 * 512).rearrange("p (g k c) -> p g k c", g=4, c=128) for i in range(2)]
    ws_sem = [P.dma_sem("ws0"), P.dma_sem("ws1")]
    Xpre = [bv(X_OFF + 18432 * i, 16 * 512).rearrange("p (k t) -> p k t", t=512) for i in range(2)]
    xp_sem = [P.dma_sem("xp0"), P.dma_sem("xp1")]
    Xm = bv(X_OFF, 16 * TM).rearrange("p (k t) -> p k t", t=TM)
    xm_sem = P.dma_sem("xm")
    xm2_sem = P.dma_sem("xm2")
    pw0_sem = P.dma_sem("pw0")
    w_sem = P.dma_sem("wfv")

    step_ctr = [0]
    NSTAGE = 7

    def make_step(h, n, xk, wf, wv, wq=None, wg=None, t0=0, pre=None):
        st = dict(h=h, n=n, xk=xk, wf=wf, wv=wv, wq=wq, wg=wg, t0=t0, i=step_ctr[0],
                  full=wq is not None, pre=pre)
        step_ctr[0] += 1
        return st

    def c3(ap, n):
        return ap[:, 0:n].rearrange("p (c j) -> p c j", j=64)

    proj_state = dict(banks=[0, 1], i=0)

    def pbank():
        bk = banks[proj_state["banks"][proj_state["i"] % len(proj_state["banks"])]]
        proj_state["i"] += 1
        return bk

    def proj(st, w, n):
        bk = pbank()
        for k in range(NK):
            mm(bk[:, 0:n], w(k), st["xk"](k), start=(k == 0), stop=(k == NK - 1))
        return bk

    def R(ringbuf, st):
        return ringbuf[st["i"] % len(ringbuf)]

    def stage0(st):
        h, n, full = st["h"], st["n"], st["full"]
        if st["pre"] is not None:
            st["pre"]()
        tA, tF = R(RA, st), R(RF, st)
        lb_h = lbt[:, h:h + 1]
        oml_h = lbt[:, 8 + h:9 + h]
        Pa = proj(st, st["wf"], n)
        act(tA[:, 0:n], Pa[:, 0:n], AF.Exp, scale=-1.0)
        Pb = proj(st, st["wv"], n)
        act(R(RV, st)[:, 0:n], Pb[:, 0:n], AF.Copy)
        act(tA[:, 0:n], tA[:, 0:n], AF.Ln, bias=1.0)
        act(tA[:, 0:n], tA[:, 0:n], AF.Exp, scale=-1.0)
        act(tF[:, 0:n], tA[:, 0:n], AF.Ln, bias=lb_h, scale=oml_h)
        if full:
            tG = R(RG, st)
            Pc = proj(st, st["wq"], n)
            cp("dve", R(RQR, st)[:, 0:n], Pc[:, 0:n])
            Pd = proj(st, st["wg"], n)
            act(tG[:, 0:n], Pd[:, 0:n], AF.Exp, scale=-1.0)
            cp("dve", R(RGR, st)[:, 0:n], Pd[:, 0:n])
            act(tG[:, 0:n], tG[:, 0:n], AF.Ln, bias=1.0)
            act(tG[:, 0:n], tG[:, 0:n], AF.Exp, scale=-1.0)

    def stage1(st):
        h, n, full = st["h"], st["n"], st["full"]
        tA, tF, tD, tB = R(RA, st), R(RF, st), R(RD, st), R(RB, st)
        P.op("dve", lambda e: e.tensor_tensor_scan(tD[:, 0:n], rmask[:, 0:n], tF[:, 0:n], 0.0, ALU.mult, ALU.add),
             reads=[rmask[:, 0:n], tF[:, 0:n]], writes=[tD[:, 0:n]])
        ts("pool", tB[:, 0:n], tA[:, 0:n], lbt[:, 16 + h:17 + h], lbt[:, 8 + h:9 + h], ALU.mult, ALU.add)
        if full:
            stt(R(RSG, st)[:, 0:n], R(RGR, st)[:, 0:n], vec[:, V_GN + h:V_GN + h + 1], R(RG, st)[:, 0:n],
                ALU.mult, ALU.mult)

    def stage2(st):
        n, full = st["n"], st["full"]
        nch = n // 64
        tD, tE, tH = R(RD, st), R(RE, st), R(RH, st)
        bl = tD[:, 63:n:64]
        tt("pool", c3(tE, n), bl.unsqueeze(2).broadcast_to([128, nch, 64]), c3(tD, n), ALU.subtract)
        act(R(REL, st)[:, 0:nch], bl, AF.Exp)
        act(tH[:, 0:n], tE[:, 0:n], AF.Exp)
        if full:
            act(R(RQ, st)[:, 0:n], tE[:, 0:n], AF.Exp, scale=-1.0)

    def stage3(st):
        n, full = st["n"], st["full"]
        tt("pool", R(RKB, st)[:, 0:n], R(RB, st)[:, 0:n], R(RH, st)[:, 0:n], ALU.mult)
        if full:
            tt("dve", R(RQB, st)[:, 0:n], R(RQR, st)[:, 0:n], R(RQ, st)[:, 0:n], ALU.mult)

    def stage4a(st):
        n = st["n"]
        nch = n // 64
        kbT, vT = R(RKB, st), R(RV, st)
        groups = [(0, min(4, nch))] + ([(4, nch)] if nch > 4 else [])
        for gi, (c0, c1) in enumerate(groups):
            bankT = banks[2 + gi].bitcast(BF16)
            for c in range(c0, c1):
                j = c - c0
                tr(bankT[0:64, j * 128:(j + 1) * 128], kbT[:, 64 * c:64 * c + 64])
                tr(bankT[0:64, 512 + j * 128:512 + (j + 1) * 128], vT[:, 64 * c:64 * c + 64])
        for gi, (c0, c1) in enumerate(groups):
            ng = c1 - c0
            bankT = banks[2 + gi].bitcast(BF16)
            src = bankT[0:64, :].rearrange("p (a b) -> p a b", a=2)[:, :, 0:ng * 128]
            dst = R(RKV[gi], st)[0:64, :].rearrange("p (a b) -> p a b", a=2)[:, :, 0:ng * 128]
            cp("act", dst, src)

    def stage4b(st):
        h, n, full = st["h"], st["n"], st["full"]
        nch = n // 64
        kbT, el = R(RKB, st), R(REL, st)
        Sbf = R(RSB, st)
        Sh = Sst[:, h * 128:(h + 1) * 128]
        groups = [(0, min(4, nch))] + ([(4, nch)] if nch > 4 else [])
        for gi, (c0, c1) in enumerate(groups):
            bankI = banks[4 + gi]
            kv = R(RKV[gi], st)
            for c in range(c0, c1):
                j = c - c0
                mm(bankI[:, j * 128:(j + 1) * 128], kv[0:64, j * 128:(j + 1) * 128],
                   kv[0:64, 512 + j * 128:512 + (j + 1) * 128])
        if full:
            qb = R(RQB, st)
            bankSc = banks[6]
            for c in range(nch):
                mm(bankSc[0:64, 64 * c:64 * c + 64], kbT[:, 64 * c:64 * c + 64], qb[:, 64 * c:64 * c + 64])
        if full:
            tt("dve", R(RSC, st)[0:64, 0:n], bankSc[0:64, 0:n], maskT[0:64, 0:n], ALU.mult)
        for gi, (c0, c1) in enumerate(groups):
            bankI = banks[4 + gi]
            for c in range(c0, c1):
                j = c - c0
                e_c = el[:, c:c + 1]
                if full:
                    ts("dve", Sbf[:, c * 128:(c + 1) * 128], Sh, e_c, None, ALU.mult)
                stt(Sh, Sh, e_c, bankI[:, j * 128:(j + 1) * 128], ALU.mult, ALU.add)

    def stage5a(st):
        n, full = st["n"], st["full"]
        if not full:
            return
        nch = n // 64
        qb, scm, Sbf = R(RQB, st), R(RSC, st), R(RSB, st)
        bankO = banks[7]
        for c in range(nch):
            j = c % 4
            mm(bankO[:, 64 * c:64 * c + 64], Sbf[:, c * 128:(c + 1) * 128], qb[:, 64 * c:64 * c + 64],
               start=True, stop=False)
            mm(bankO[:, 64 * c:64 * c + 64], R(RKV[c // 4], st)[0:64, 512 + j * 128:512 + (j + 1) * 128],
               scm[0:64, 64 * c:64 * c + 64], start=False, stop=True)
        act(R(RSQ, st)[:, 0:n], bankO[:, 0:n], AF.Square)

    def stage5b(st):
        h, n, full, t0 = st["h"], st["n"], st["full"], st["t0"]
        if not full:
            return
        sq, tL, tO = R(RSQ, st), R(RL, st), R(RO, st)
        bankO = banks[7]
        bankM = pbank()
        mm(bankM[:, 0:n], onesd[:], sq[:, 0:n])
        act(tL[:, 0:n], bankM[:, 0:n], AF.Ln, bias=RMS_EPS)
        act(tL[:, 0:n], tL[:, 0:n], AF.Exp, scale=-0.5)
        tt("dve", tO[:, 0:n], bankO[:, 0:n], tL[:, 0:n], ALU.mult)
        tt("dve", mixT[:, 8 + h, t0:t0 + n], tO[:, 0:n], R(RSG, st)[:, 0:n], ALU.mult)

    TICK = [(stage4a, 4), (stage5b, 6), (stage3, 3), (stage2, 2), (stage1, 1), (stage0, 0), (stage4b, 4), (stage5a, 5)]

    def run_steps(steps):
        ns = len(steps)
        for t in range(ns + NSTAGE - 1):
            for fn, j in TICK:
                i = t - j
                if 0 <= i < ns:
                    fn(steps[i])

    def load_pre_tile(ti):
        t0, t1 = PRE_TILES[ti]
        n = t1 - t0
        fill([(Xpre[ti % 2][:, :, 0:n], kview(xT_h[:, t0:t1]))], xp_sem[ti % 2])

    def finish_early():
        osem = P.dma_sem("out")
        P.dma("pool", outT_h[0:128, :], fv(X_OFF, 1024), osem, reads=[fv(X_OFF, 1024)], final=True)
        if debug:
            dsem = P.dma_sem("dbg")
            P.dma("pool", dbg_h, bv(R_OFF, 16 * TM), dsem, reads=[bv(R_OFF, 16 * TM)], final=True)
        P.emit()
        return nc, P

    if stop_after == "consts":
        return finish_early()
    steps = []
    load_pre_tile(0)
    for h in range(NH):
        fill([(Wf[:, h, :, :], kview(w_in_h[:, 2048 + h * 128:2048 + (h + 1) * 128])),
              (Wv[:, h, :, :], kview(w_in_h[:, 3072 + h * 128:3072 + (h + 1) * 128]))], wh_sem[h])
    pre_tiles = PRE_TILES[:1] if stop_after == "A1" else PRE_TILES
    for ti, (t0, t1) in enumerate(pre_tiles):
        n = t1 - t0
        xb = Xpre[ti % 2]
        for h in range(NH):
            pre = None
            if h == 0 and ti + 1 < len(pre_tiles):
                pre = (lambda ti=ti: load_pre_tile(ti + 1))
            if ti == len(PRE_TILES) - 1 and h == 1:
                pre = (lambda: fill([(Xm[:, 0:8, :], kview(xT_h[0:1024, TP:TP + TM]))], xm_sem))
            if ti == len(PRE_TILES) - 1 and h == 6:
                pre = (lambda: fill([(WS[0][:, :, :], kview(w_in_h[:, 0:512]))], pw0_sem))
            steps.append(make_step(
                h, n,
                xk=(lambda k, xb=xb, n=n: xb[:, k, 0:n]),
                wf=(lambda k, h=h: Wf[:, h, k, :]),
                wv=(lambda k, h=h: Wv[:, h, k, :]), pre=pre))
    if stop_after == "A1load":
        return finish_early()
    proj_state["banks"] = [0, 1, 6, 7]
    stepsA = steps

    def b_prologue():
        proj_state["banks"] = [0, 1]
        fill([(Xm[:, 8:16, :], kview(xT_h[1024:2048, TP:TP + TM]))], xm2_sem)
        fill([(WS[1][:, :, :], kview(w_in_h[:, 512:1024]))], ws_sem[1])
        o = PT_OFF
        PADW = 16 + TM
        U0s = [fv(o, PADW), fv(o + 4 * PADW, PADW)]; o += 8 * PADW
        A1s = [fv(o, PADW), fv(o + 4 * PADW, PADW)]; o += 8 * PADW
        A2s = [fv(o, PADW), fv(o + 4 * PADW, PADW)]; o += 8 * PADW
        PB = [bv(o, TM), bv(o + 2 * TM, TM)]; o += 4 * TM
        PW = bv(o, 8 * 256).rearrange("p (r d) -> p r d", d=256); o += 4096
        t16 = fv(o, 16); o += 64
        assert o <= ARENA_BYTES, o
        pw_sem = P.dma_sem("pw")
        fill([(PW[:, :, :], pool_w_h.rearrange("(r p) d -> p r d", p=128))], pw_sem)
        for buf in U0s + A1s + A2s:
            P.op("pool", (lambda e, buf=buf: e.memset(buf[:, 0:16], 0.0)), writes=[buf[:, 0:16]])

        for cc in range(8):
            g = cc // 2
            w = WINDOWS[g]
            ws = WS[cc // 4]
            col = (cc % 4) * 128
            U0, A1, A2 = U0s[cc % 2], A1s[cc % 2], A2s[cc % 2]
            for (t0, t1) in MAIN_TILES:
                n = t1 - t0
                bk = next_proj()
                for k in range(NK):
                    mm(bk[:, 0:n], ws[:, k, col:col + 128], Xm[:, k, t0:t1], start=(k == 0), stop=(k == NK - 1))
                act(U0[:, 16 + t0:16 + t1], bk[:, 0:n], AF.Copy)
            cur = U0
            seq = [A1, A2, A1, A2]
            sh = 1
            i = 0
            while sh < w:
                nxt = seq[i]
                tt("dve", nxt[:, 16:16 + TM], cur[:, 16:16 + TM], cur[:, 16 - sh:16 - sh + TM], ALU.add)
                cur = nxt
                sh *= 2
                i += 1
            pb = PB[cc % 2]
            stt(pb[:, 0:TM], cur[:, 16:16 + TM], 1.0 / w, U0[:, 16:16 + TM], ALU.mult, ALU.subtract)
            tt("dve", t16[:, 0:16], cur[:, 16 + 64:16 + 80], invc[:, g * 16:(g + 1) * 16], ALU.mult)
            tt("dve", pb[:, 64:80], t16[:, 0:16], U0[:, 16 + 64:16 + 80], ALU.subtract)
            if cc % 2 == 1:
                for dc in range(2):
                    oc = g * 2 + dc
                    for (t0, t1) in MAIN_TILES:
                        n = t1 - t0
                        bk = next_proj()
                        for ci in range(2):
                            mm(bk[:, 0:n], PW[:, g * 2 + ci, dc * 128:(dc + 1) * 128], PB[ci][:, t0:t1],
                               start=(ci == 0), stop=(ci == 1))
                        ts("dve", mixT[:, oc, t0:t1], bk[:, 0:n], vec[:, V_PB + oc:V_PB + oc + 1],
                           vec[:, V_PS + oc:V_PS + oc + 1], ALU.add, ALU.mult)

        load_head(0)

    if stop_after == "poolmix":
        return finish_early()

    def load_head(h):
        slot = WS4[h % 2]
        fill([(slot[:, gi, :, :], kview(w_in_h[:, base + h * 128:base + (h + 1) * 128]))
              for gi, base in enumerate((1024, 2048, 3072, 4096))], ws_sem[h % 2])

    if stop_after == "B1load":
        load_head(0)
        mm(banks[0][:, 0:512], WS4[0][:, 0, 0, :], Xm[:, 0, 0:512])
        return finish_early()
    steps = []
    nheads = 1 if stop_after in ("B1", "B1a") else NH
    for h in range(nheads):
        slot = WS4[h % 2]
        for tix, (t0, t1) in enumerate(MAIN_TILES[:1] if stop_after == "B1a" else MAIN_TILES):
            n = t1 - t0
            pre = None
            if tix == 0 and h + 1 < nheads:
                pre = (lambda h=h: load_head(h + 1))
            if tix == 0 and h == 0:
                pre = (lambda: (b_prologue(), load_head(1)))
            steps.append(make_step(
                h, n,
                xk=(lambda k, t0=t0, t1=t1: Xm[:, k, t0:t1]),
                wq=(lambda k, slot=slot: slot[:, 0, k, :]),
                wf=(lambda k, slot=slot: slot[:, 1, k, :]),
                wv=(lambda k, slot=slot: slot[:, 2, k, :]),
                wg=(lambda k, slot=slot: slot[:, 3, k, :]),
                t0=t0, pre=pre))
    run_steps(stepsA + steps)

    if stop_after in ("mixer", "B1", "B1a"):
        return finish_early()

    Z1_OFFS = [X_OFF + 4352 * d for d in range(8)] + [TMP_OFF + 4352 * d for d in range(8)]
    z1 = [fv(Z1_OFFS[d], TM) for d in range(16)]
    o = TMP_OFF + 8 * 4352
    XR = [fv(o, TM), fv(o + 4352, TM)]; o += 8704
    XF = [fv(o, TM), fv(o + 4352, TM)]; o += 8704
    sqx = [fv(o, 512), fv(o + 2048, 512)]; o += 4096
    assert o <= ARENA_BYTES, o
    xr_sem = [P.dma_sem("xr0"), P.dma_sem("xr1")]
    mu = fv(R_OFF + 34816, TM)
    rs = fv(R_OFF + 34816 + 4352, TM)
    msq = fv(R_OFF + 34816 + 8704, TM)
    x1T = bv(R_OFF, 16 * T1).rearrange("p (k t) -> p k t", t=T1)

    def load_wout(g):
        fill([(WS[g % 2][:, :, :], kview(w_out_h[:, g * 512:(g + 1) * 512]))], ws_sem[g % 2])

    acc1 = fv(o, TM); o += 4 * TM
    acc2 = fv(o, TM); o += 4 * TM
    assert o <= ARENA_BYTES, o
    sq_i = [0]

    def stat_acc(zap, a1, a2, first, e1="pool", e2="pool"):
        sb = sqx[sq_i[0] % 2]
        sq_i[0] += 1
        n_ = zap.shape[1]
        if first:
            cp(e1, a1, zap)
            act(a2, zap, AF.Square)
        else:
            act(sb[:, 0:n_], zap, AF.Square)
            tt(e1, a1, a1, zap, ALU.add)
            tt(e2, a2, a2, sb[:, 0:n_], ALU.add)

    def ln_finish(a1, a2, tiles, mu_, rs_, msq_, eps):
        for (t0, t1) in tiles:
            n = t1 - t0
            b0 = next_proj()
            b1 = next_proj()
            mm(b0[:, 0:n], onesf[:], a1[:, t0:t1])
            mm(b1[:, 0:n], onesf[:], a2[:, t0:t1])
            ts("dve", mu_[:, t0:t1], b0[:, 0:n], 1.0 / D, None, ALU.mult)
            tt("dve", msq_[:, t0:t1], mu_[:, t0:t1], mu_[:, t0:t1], ALU.mult)
            stt(msq_[:, t0:t1], b1[:, 0:n], 1.0 / D, msq_[:, t0:t1], ALU.mult, ALU.subtract)
            act(rs_[:, t0:t1], msq_[:, t0:t1], AF.Ln, bias=eps)
            act(rs_[:, t0:t1], rs_[:, t0:t1], AF.Exp, scale=-0.5)

    load_wout(0)
    for g in range(4):
        if g + 1 < 4:
            load_wout(g + 1)
        for dc in range(4):
            d = 4 * g + dc
            xr = XR[d % 2]
            P.dma("sp", xr[:, 0:TM], xT_h[d * 128:(d + 1) * 128, TP:TP + TM], xr_sem[d % 2], writes=[xr[:, 0:TM]])
            for (t0, t1) in MAIN_TILES:
                n = t1 - t0
                bk = next_proj()
                for k in range(NK):
                    mm(bk[:, 0:n], WS[g % 2][:, k, dc * 128:(dc + 1) * 128], mixT[:, k, t0:t1],
                       start=(k == 0), stop=(k == NK - 1))
                stt(z1[d][:, t0:t1], xr[:, t0:t1], ALPHA, bk[:, 0:n], ALU.mult, ALU.add)
                stat_acc(z1[d][:, t0:t1], acc1[:, t0:t1], acc2[:, t0:t1], d == 0, e1="dve", e2="dve")

    ln_finish(acc1, acc2, MAIN_TILES, mu, rs, msq, LN_EPS)
    x1_sem = [P.dma_sem("x1s0"), P.dma_sem("x1s1")]
    WD_OFF = ARENA_BYTES - 16384
    assert o <= WD_OFF
    WD = [bv(WD_OFF + 4096 * i, 16 * 128).rearrange("p (k c) -> p k c", c=128) for i in range(4)]
    wd_sem = [P.dma_sem("wd0"), P.dma_sem("wd1")]

    def load_wup(p):
        fill([(WD[(p % 2) * 2 + half][:, :, :], kview(w_up_h[:, half * DFF + p * 128:half * DFF + (p + 1) * 128]))
              for half in range(2)], wd_sem[p % 2])

    load_wup(0)
    load_wup(1)
    nb1 = msq
    stt(nb1[:, 0:TM], mu[:, 0:TM], -1.0, rs[:, 0:TM], ALU.mult, ALU.mult)
    for d in range(16):
        tt("dve", z1[d][:, 0:TM], z1[d][:, 0:TM], rs[:, 0:TM], ALU.mult)
        tt("dve", z1[d][:, 0:TM], z1[d][:, 0:TM], nb1[:, 0:TM], ALU.add)
        act(x1T[:, d, 0:2], z1[d][:, 62:64], AF.Identity, bias=gfb[:, 16 + d:17 + d], scale=gfb[:, d:d + 1])
        act(x1T[:, d, 2:T1], z1[d][:, 64:TM], AF.Identity, bias=vec[:, V_B1 + d:V_B1 + d + 1],
            scale=vec[:, V_G1 + d:V_G1 + d + 1])
    XF4 = [XF[0], XF[1], XR[0], XR[1]]
    x1_sem4 = x1_sem + [P.dma_sem("x1s2"), P.dma_sem("x1s3")]
    for d in range(16):
        xf = XF4[d % 4]
        ts("dve", xf[:, 0:TO], z1[d][:, 64:TM], vec[:, V_G1 + d:V_G1 + d + 1], vec[:, V_B1 + d:V_B1 + d + 1],
           ALU.mult, ALU.add)
        P.dma("sp", x1s_h[d * 128:(d + 1) * 128, :], xf[:, 0:TO], x1_sem4[d % 4],
              reads=[xf[:, 0:TO]], writes=[("x1s", d)])

    if stop_after == "ln1":
        if debug:
            dsem = P.dma_sem("dbg")
            P.dma("pool", dbg_h[:, 0:16 * T1], bv(R_OFF, 16 * T1), dsem, reads=[bv(R_OFF, 16 * T1)], final=True)
        osem = P.dma_sem("out")
        P.dma("pool", outT_h[0:128, :], fv(X_OFF, 1024), osem, reads=[fv(X_OFF, 1024)], final=True)
        P.emit()
        return nc, P

    HT_OFF = R_OFF + 2 * 16 * T1
    assert HT_OFF + NFF * TO * 2 <= ARENA_BYTES - 4112, HT_OFF
    hT = bv(HT_OFF, NFF * TO).rearrange("p (k t) -> p k t", t=TO)
    o = X_OFF
    UR = [fv(o, 1028), fv(o + 4112, 1028)]; o += 8224
    ACC = [fv(o, 1028), fv(o + 4112, 1028)]; o += 8224
    assert o <= R_OFF, o
    SG = fv(HT_OFF + NFF * TO * 2, 1028)
    assert HT_OFF + NFF * TO * 2 + 4112 <= ARENA_BYTES

    for p in range(NFF):
        if 2 <= p + 1 < NFF:
            load_wup(p + 1)
        for half in range(2):
            wd = WD[(p % 2) * 2 + half]
            cidx = half * NFF + p
            bks = [next_proj() for _ in FF_TILES]
            for k in range(NK):
                for ti, (t0, t1) in enumerate(FF_TILES):
                    mm(bks[ti][:, 0:t1 - t0], wd[:, k, :], x1T[:, k, t0:t1], start=(k == 0), stop=(k == NK - 1))
            ur = UR[half]
            for ti, (t0, t1) in enumerate(FF_TILES):
                act(ur[:, t0:t1], bks[ti][:, 0:t1 - t0], AF.Copy)
            acc = ACC[half]
            cw = lambda j: vec[:, V_CW + 88 * j + cidx:V_CW + 88 * j + cidx + 1]
            ts("dve", acc[:, 0:TO], ur[:, 2:2 + TO], cw(2), vec[:, V_CB + cidx:V_CB + cidx + 1], ALU.mult, ALU.add)
            stt(acc[:, 0:TO], ur[:, 1:1 + TO], cw(1), acc[:, 0:TO], ALU.mult, ALU.add)
            stt(acc[:, 0:TO], ur[:, 0:TO], cw(0), acc[:, 0:TO], ALU.mult, ALU.add)
            if half == 0:
                act(SG[:, 0:TO], acc[:, 0:TO], AF.Silu)
            else:
                tt("dve", hT[:, p, :], SG[:, 0:TO], acc[:, 0:TO], ALU.mult)

    NWR = 8
    WR_OFF = X_OFF + 4096 * 14
    Z2_OFFS = [X_OFF + 4096 * d for d in range(14)] + [HT_OFF + 4096 * d for d in range(2)]
    z2 = [fv(Z2_OFFS[d], TO) for d in range(16)]
    WR = [bv(WR_OFF + 1024 * i, 2 * 256).rearrange("p (a c) -> p a c", c=256) for i in range(NWR)]
    assert WR_OFF + 1024 * NWR <= HT_OFF
    XS = [fv(STG_OFF, 1024), fv(STG_OFF + 4096, 1024)]
    wr_sem = [P.dma_sem("wr%d" % i) for i in range(NWR)]
    xs_sem = [P.dma_sem("xs0"), P.dma_sem("xs1")]
    OF = [fv(HT_OFF + 8192, TO), fv(HT_OFF + 8192 + 4096, TO), fv(HT_OFF + 28672, TO), fv(HT_OFF + 32768, TO)]
    of_sem = [P.dma_sem("of%d" % i) for i in range(4)]
    sqx = [fv(HT_OFF + NFF * TO * 2, 512), fv(HT_OFF + NFF * TO * 2 + 2048, 512)]
    acc1 = fv(HT_OFF + NFF * TO * 2 + 4096, TO)
    acc2 = fv(HT_OFF + NFF * TO * 2 + 8192, TO)
    assert HT_OFF + NFF * TO * 2 + 12288 <= ARENA_BYTES
    mu2 = fv(HT_OFF + 16384, TO)
    rs2 = fv(HT_OFF + 16384 + 4096, TO)
    msq2 = fv(HT_OFF + 16384 + 8192, TO)
    E_TILES = [(0, 512), (512, 1024)]
    wr_i = 0
    for g in range(8):
        accb = [banks[(g % 2) * 4 + i] for i in range(4)]
        for k2 in range(NFF // 2):
            slot = WR[wr_i % NWR]
            src = w_down_h[k2 * 256:(k2 + 1) * 256, g * 256:(g + 1) * 256].rearrange("(a p) c -> p a c", p=128)
            fill([(slot[:, :, :], src)], wr_sem[wr_i % NWR])
            wr_i += 1
            for a in range(2):
                k = 2 * k2 + a
                for dc in range(2):
                    for ti, (t0, t1) in enumerate(E_TILES):
                        mm(accb[dc * 2 + ti][:, 0:512], slot[:, a, dc * 128:(dc + 1) * 128], hT[:, k, t0:t1],
                           start=(k == 0), stop=(k == NFF - 1))
        for dc in range(2):
            d = 2 * g + dc
            xs = XS[d % 2]
            P.dma("sp", xs[:, 0:TO], x1s_h[d * 128:(d + 1) * 128, :], xs_sem[d % 2],
                  reads=[("x1s", d)], writes=[xs[:, 0:TO]])
            for ti, (t0, t1) in enumerate(E_TILES):
                stt(z2[d][:, t0:t1], xs[:, t0:t1], ALPHA, accb[dc * 2 + ti][:, 0:512], ALU.mult, ALU.add)
        for dc in range(2):
            d = 2 * g + dc
            for ti, (t0, t1) in enumerate(E_TILES):
                stat_acc(z2[d][:, t0:t1], acc1[:, t0:t1], acc2[:, t0:t1], d == 0, e1="dve", e2="dve")
    ln_finish(acc1, acc2, E_TILES, mu2, rs2, msq2, LN_EPS)
    nb2 = msq2
    stt(nb2[:, 0:TO], mu2[:, 0:TO], -1.0, rs2[:, 0:TO], ALU.mult, ALU.mult)
    for d in range(16):
        tt("dve", z2[d][:, 0:TO], z2[d][:, 0:TO], rs2[:, 0:TO], ALU.mult)
        tt("dve", z2[d][:, 0:TO], z2[d][:, 0:TO], nb2[:, 0:TO], ALU.add)
        of = OF[d % 4]
        act(of[:, 0:TO], z2[d][:, 0:TO], AF.Identity, bias=vec[:, V_B2 + d:V_B2 + d + 1],
            scale=vec[:, V_G2 + d:V_G2 + d + 1])
        P.dma("sp", outT_h[d * 128:(d + 1) * 128, :], of[:, 0:TO], of_sem[d % 4], reads=[of[:, 0:TO]], final=True)
    P.emit()
    return nc, P


def _cols(v, n):
    return np.ascontiguousarray(np.asarray(v, np.float32).reshape(n, 128).T)


def prep_inputs(x, w_in, pool_w, pool_b, pool_scale, hgrn_lb_logits, hgrn_g_norm, w_out,
                ln1_g, ln1_b, w_up, conv_w, conv_b, w_down, ln2_g, ln2_b):
    f32 = lambda a: np.ascontiguousarray(np.asarray(a, np.float32))
    x = f32(x)
    vec = np.zeros((128, NV), np.float32)
    vec[:, V_PB:V_PB + 8] = _cols(np.asarray(pool_b)[0].reshape(-1), 8)
    vec[:, V_PS:V_PS + 8] = _cols(np.asarray(pool_scale)[0], 8)
    vec[:, V_L0:V_L0 + 8] = _cols(np.asarray(hgrn_lb_logits)[0], 8)
    vec[:, V_L1:V_L1 + 8] = _cols(np.asarray(hgrn_lb_logits)[1], 8)
    vec[:, V_GN:V_GN + 8] = _cols(np.asarray(hgrn_g_norm)[0], 8)
    cw = np.asarray(conv_w, np.float32)[0]
    for j in range(3):
        vec[:, V_CW + 88 * j:V_CW + 88 * (j + 1)] = _cols(cw[j], 88)
    vec[:, V_CB:V_CB + 88] = _cols(np.asarray(conv_b)[0], 88)
    vec[:, V_G1:V_G1 + 16] = _cols(np.asarray(ln1_g)[0], 16)
    vec[:, V_B1:V_B1 + 16] = _cols(np.asarray(ln1_b)[0], 16)
    vec[:, V_G2:V_G2 + 16] = _cols(np.asarray(ln2_g)[0], 16)
    vec[:, V_B2:V_B2 + 16] = _cols(np.asarray(ln2_b)[0], 16)
    ident = np.eye(128, dtype=np.float32).astype(ml_dtypes.bfloat16)
    m = np.triu(np.ones((64, 64), np.float32))
    maskT = np.ascontiguousarray(np.tile(m, (1, 8)))
    rmask = np.ones((1, 512), np.float32)
    rmask[0, ::64] = 0.0
    shared = dict(
        w_in=f32(w_in)[0], pool_w=f32(pool_w)[0].reshape(1024, 256), w_out=f32(w_out)[0],
        w_up=f32(w_up)[0], w_down=f32(w_down)[0], vec=vec, ident=ident, maskT=maskT, rmask=rmask)
    in_maps = []
    for c in range(8):
        b, j = divmod(c, 4)
        npre = 1024 * (j + 1)
        xcat = np.zeros((SEQ, D), np.float32)
        xcat[SEQ - npre:] = x[b, :npre]
        invc = np.zeros((1, 64), np.float32)
        for g, w in enumerate(WINDOWS):
            for i in range(16):
                invc[0, g * 16 + i] = 1.0 / min(1024 * j + i + 1, w)
        flag = np.full((1, 1), 0.0 if j == 0 else 1.0, np.float32)
        m_ = dict(shared)
        m_.update(xT=np.ascontiguousarray(xcat.T), invc=invc, flag=flag)
        in_maps.append(m_)
    return in_maps


def kernel(**inputs):
    in_maps = prep_inputs(**inputs)
    nc, _ = build_program()
    res = run_bass_kernel_spmd(nc, in_maps, core_ids=list(range(8)))
    out = np.zeros((2, SEQ, D), np.float32)
    for c in range(8):
        b, j = divmod(c, 4)
        out[b, 1024 * j:1024 * (j + 1), :] = res.results[c]["outT"].T
    return out
```

```python
import numpy as np
import ml_dtypes
import concourse.bass as bass
import concourse.mybir as mybir
from concourse.bass_utils import run_bass_kernel_spmd

F32 = mybir.dt.float32
BF16 = mybir.dt.bfloat16
AF = mybir.ActivationFunctionType
ALU = mybir.AluOpType


class _Op:
    __slots__ = ("eng", "fn", "deps", "is_dma", "sem", "val", "signal", "idx", "final", "inc")

    def __init__(self, eng, fn):
        self.eng = eng
        self.fn = fn
        self.deps = []
        self.is_dma = False
        self.sem = None
        self.val = 0
        self.signal = False
        self.idx = 0
        self.final = False
        self.inc = 16


class Prog:
    ENGS = ("pe", "act", "dve", "pool", "sp")

    def __init__(self, nc):
        self.nc = nc
        self.ops = {e: [] for e in self.ENGS}
        self.recs = {}
        self.esem = {e: nc.alloc_semaphore(name="ctr_" + e) for e in ("pe", "act", "dve", "pool")}
        self.dma_sems = {}
        self.dma_cnt = {}
        self.finals = []

    def sbuf(self, name, shape, dtype):
        return self.nc.alloc_sbuf_tensor("sb_" + name, shape, dtype)

    def psum(self, name, shape, dtype):
        return self.nc.alloc_psum_tensor("ps_" + name, shape, dtype)

    def dma_sem(self, name):
        s = self.nc.alloc_semaphore(name="dma_" + name)
        self.dma_cnt[id(s)] = 0
        return s

    @staticmethod
    def _region(a):
        if isinstance(a, tuple):
            return (a, 0, 1)
        cls = type(a.tensor).__name__
        if cls == "PSumTensorHandle":
            return (a.tensor.name, 0, 2048, True)
        if cls != "SBTensorHandle":
            return None
        es = mybir.dt.size(a.dtype)
        ap = a.ap
        row = ap[0][0]
        off = a.offset % row if row > 0 else a.offset
        ext = 1
        for st, cnt in ap[1:]:
            ext += (cnt - 1) * abs(st)
        return (a.tensor.name, off * es, (off + ext) * es)

    def _access(self, op, reg, is_write, deps):
        name, lo, hi = reg[0], reg[1], reg[2]
        if len(reg) > 3:
            is_write = True
        recs = self.recs.get(name, [])
        new = []
        for r in recs:
            rlo, rhi, rop, rw = r
            if rhi <= lo or hi <= rlo:
                new.append(r)
                continue
            if (rw or is_write) and rop is not op:
                deps.append(rop)
            if is_write and lo <= rlo and rhi <= hi:
                continue
            if (not is_write) and (not rw) and rlo == lo and rhi == hi and rop.eng == op.eng \
                    and not rop.is_dma and not op.is_dma:
                continue
            new.append(r)
        new.append((lo, hi, op, is_write))
        self.recs[name] = new

    def _track(self, op, reads, writes):
        deps = []
        for a in reads:
            reg = self._region(a)
            if reg is not None:
                self._access(op, reg, False, deps)
        for a in writes:
            reg = self._region(a)
            if reg is not None:
                self._access(op, reg, True, deps)
        seen = set()
        for d in deps:
            if d is op or id(d) in seen:
                continue
            seen.add(id(d))
            op.deps.append(d)

    def op(self, eng, fn, reads=(), writes=()):
        o = _Op(eng, fn)
        self._track(o, reads, writes)
        self.ops[eng].append(o)
        return o

    def dma(self, eng, out, in_, sem, reads=(), writes=(), final=False, **kw):
        o = _Op(eng, lambda e: e.dma_start(out=out, in_=in_, **kw))
        o.is_dma = True
        o.sem = sem
        self.dma_cnt[id(sem)] += 16
        o.val = self.dma_cnt[id(sem)]
        o.final = final
        self._track(o, reads, writes)
        self.ops[eng].append(o)
        if final:
            self.finals.append(o)
        return o

    def emit(self):
        nc = self.nc
        for e in self.ENGS:
            for o in self.ops[e]:
                for d in o.deps:
                    if d.is_dma:
                        continue
                    if d.eng == "pe" and o.eng == "pe" and not o.is_dma:
                        continue
                    d.signal = True
        for e in self.ENGS:
            n = 0
            for o in self.ops[e]:
                if o.is_dma:
                    continue
                if o.signal:
                    n += 1
                    o.idx = n
        self.sig_counts = {e: sum(1 for o in self.ops[e] if o.signal) for e in self.ENGS}
        handles = {"pe": "tensor", "act": "scalar", "dve": "vector", "pool": "gpsimd", "sp": "sync"}

        def run(ename, eng):
            waited = {}
            for o in self.ops[ename]:
                for d in o.deps:
                    if d.is_dma:
                        sem, val = d.sem, d.val
                    else:
                        if d.eng == "pe" and ename == "pe" and not o.is_dma:
                            continue
                        sem, val = self.esem[d.eng], d.idx
                    if waited.get(id(sem), 0) >= val:
                        continue
                    waited[id(sem)] = val
                    eng.wait_ge(sem, val)
                ins = o.fn(eng)
                if o.is_dma:
                    ins.then_inc(o.sem, o.inc)
                elif o.signal:
                    ins.then_inc(self.esem[ename], 1)
            if ename == "sp":
                for o in self.finals:
                    if waited.get(id(o.sem), 0) >= o.val:
                        continue
                    waited[id(o.sem)] = o.val
                    eng.wait_ge(o.sem, o.val)

        with nc.Block() as block:
            for ename in self.ENGS:
                def mk(ename):
                    def f(eng):
                        run(ename, eng)
                    return f
                getattr(block, handles[ename])(mk(ename))


D = 2048
NK = 16
SEQ = 4096
TP = 3008
TM = 1088
TO = 1024
DFF = 5632
NFF = 44
NH = 8
ALPHA = 2.0 ** 0.25
LN_EPS = 1e-5
RMS_EPS = 1e-6
WINDOWS = (2, 4, 8, 16)
PRE_TILES = [(0, 512), (512, 1024), (1024, 1536), (1536, 2048), (2048, 2560), (2560, 3008)]
MAIN_TILES = [(0, 384), (384, 768), (768, 1088)]
T1 = 1026
FF_TILES = [(0, 342), (342, 684), (684, 1026)]

V_PB, V_PS, V_L0, V_L1, V_GN = 0, 8, 16, 24, 32
V_CW, V_CB = 40, 304
V_G1, V_B1, V_G2, V_B2 = 392, 408, 424, 440
NV = 456

ARENA_BYTES = 200704
STG_OFF = 0
X_OFF = 8192
R_OFF = 43008
TMP_OFF = 112640
PT_OFF = 157696


def build_program(stop_after=None, debug=False):
    nc = bass.Bass("TRN2", target_bir_lowering=False)
    P = Prog(nc)

    def dram(name, shape, dt=F32, kind="ExternalInput"):
        return nc.dram_tensor(name, shape, dt, kind=kind).ap()

    xT_h = dram("xT", [D, SEQ])
    w_in_h = dram("w_in", [D, 5120])
    pool_w_h = dram("pool_w", [1024, 256])
    w_out_h = dram("w_out", [D, D])
    w_up_h = dram("w_up", [D, 2 * DFF])
    w_down_h = dram("w_down", [DFF, D])
    vec_h = dram("vec", [128, NV])
    ident_h = dram("ident", [128, 128], BF16)
    maskT_h = dram("maskT", [64, 512])
    rmask_h = dram("rmask", [1, 512])
    invc_h = dram("invc", [1, 64])
    flag_h = dram("flag", [1, 1])
    outT_h = dram("outT", [D, TO], kind="ExternalOutput")
    x1s_h = dram("x1s", [D, TO], kind="Internal")
    dbg_h = None
    if debug:
        dbg_h = dram("dbg", [128, 16 * TM], BF16, kind="ExternalOutput")

    vec = P.sbuf("vec", [128, NV], F32)
    ident = P.sbuf("ident", [128, 128], BF16)
    maskT = P.sbuf("maskT", [64, 512], F32)
    rmask = P.sbuf("rmask", [128, 512], F32)
    invc = P.sbuf("invc", [128, 64], F32)
    flag = P.sbuf("flag", [128, 1], F32)
    lbt = P.sbuf("lbt", [128, 24], F32)
    Sst = P.sbuf("Sst", [128, NH * 128], F32)
    onesf = P.sbuf("onesf", [128, 128], F32)
    onesd = P.sbuf("onesd", [128, 128], F32)
    arena = P.sbuf("arena", [128, ARENA_BYTES // 4], F32)
    arena_bf = arena.bitcast(BF16)
    banks = [P.psum("bank%d" % i, [128, 512], F32) for i in range(8)]

    def fv(off, n):
        assert off % 4 == 0 and off + 4 * n <= ARENA_BYTES, (off, n)
        return arena[:, off // 4: off // 4 + n]

    def bv(off, n):
        assert off % 2 == 0 and off + 2 * n <= ARENA_BYTES, (off, n)
        return arena_bf[:, off // 2: off // 2 + n]

    def mm(out, lhsT, rhs, start=True, stop=True):
        P.op("pe", lambda e: e.matmul(out, lhsT, rhs, start=start, stop=stop),
             reads=[lhsT, rhs], writes=[out])

    def tr(out, in_):
        P.op("pe", lambda e: e.transpose(out, in_, ident[:]), reads=[in_, ident[:]], writes=[out])

    def act(out, in_, func, bias=None, scale=1.0, extra_reads=()):
        rd = [in_] + list(extra_reads)
        kw = {}
        if bias is not None:
            kw["bias"] = bias
            if not isinstance(bias, (int, float)):
                rd.append(bias)
        if not isinstance(scale, (int, float)):
            rd.append(scale)
        P.op("act", lambda e: e.activation(out, in_, func, scale=scale, **kw), reads=rd, writes=[out])

    def ts(eng, out, in0, s1, s2, op0, op1=None):
        rd = [in0] + [s for s in (s1, s2) if s is not None and not isinstance(s, (int, float))]
        if op1 is None:
            P.op(eng, lambda e: e.tensor_scalar(out, in0, s1, None, op0), reads=rd, writes=[out])
        else:
            P.op(eng, lambda e: e.tensor_scalar(out, in0, s1, s2, op0, op1), reads=rd, writes=[out])

    def stt(out, in0, scalar, in1, op0, op1):
        rd = [in0, in1] + ([] if isinstance(scalar, (int, float)) else [scalar])
        P.op("dve", lambda e: e.scalar_tensor_tensor(out, in0, scalar, in1, op0, op1), reads=rd, writes=[out])

    def tt(eng, out, in0, in1, op):
        P.op(eng, lambda e: e.tensor_tensor(out, in0, in1, op), reads=[in0, in1], writes=[out])

    def cp(eng, out, in_):
        if eng == "act":
            act(out, in_, AF.Copy)
        else:
            P.op(eng, lambda e: e.tensor_copy(out, in_), reads=[in_], writes=[out])

    def fill(pairs, sem, queue="pool"):
        ops_ = [P.dma(queue, dst, src, sem, writes=[dst]) for dst, src in pairs]
        tot = P.dma_cnt[id(sem)]
        for o_ in ops_:
            o_.val = tot

    def kview(h_ap):
        return h_ap.rearrange("(k p) c -> p k c", p=128)

    csem = P.dma_sem("const")
    P.dma("sp", vec[:], vec_h, csem, writes=[vec[:]])
    P.dma("sp", ident[:], ident_h, csem, writes=[ident[:]])
    P.dma("sp", maskT[:], maskT_h, csem, writes=[maskT[:]])
    P.dma("sp", rmask[:], rmask_h.partition_broadcast(128), csem, writes=[rmask[:]])
    P.dma("sp", invc[:], invc_h.partition_broadcast(128), csem, writes=[invc[:]])
    P.dma("sp", flag[:], flag_h.partition_broadcast(128), csem, writes=[flag[:]])
    for o_ in P.ops["sp"]:
        if o_.is_dma and o_.sem is csem:
            o_.val = P.dma_cnt[id(csem)]
    P.op("pool", lambda e: e.memset(onesf[:], 1.0), writes=[onesf[:]])
    P.op("pool", lambda e: e.memset(onesd[:], 1.0 / 128.0), writes=[onesd[:]])
    P.op("pool", lambda e: e.memset(Sst[:], 0.0), writes=[Sst[:]])
    tt("dve", lbt[:, 0:8], vec[:, V_L1:V_L1 + 8], vec[:, V_L0:V_L0 + 8], ALU.subtract)
    act(lbt[:, 0:8], lbt[:, 0:8], AF.Exp)
    ts("dve", lbt[:, 0:8], lbt[:, 0:8], 1.0, None, ALU.add)
    P.op("dve", lambda e: e.reciprocal(lbt[:, 0:8], lbt[:, 0:8]), reads=[lbt[:, 0:8]], writes=[lbt[:, 0:8]])
    ts("dve", lbt[:, 8:16], lbt[:, 0:8], -1.0, 1.0, ALU.mult, ALU.add)
    ts("dve", lbt[:, 16:24], lbt[:, 0:8], -1.0, None, ALU.add)
    gfb = P.sbuf("gfb", [128, 32], F32)
    ts("dve", gfb[:, 0:16], vec[:, V_G1:V_G1 + 16], flag[:, 0:1], None, ALU.mult)
    ts("dve", gfb[:, 16:32], vec[:, V_B1:V_B1 + 16], flag[:, 0:1], None, ALU.mult)

    pb_i = [0]

    def next_proj():
        b = banks[pb_i[0] % 8]
        pb_i[0] += 1
        return b

    tmp_off = [TMP_OFF]

    def ring(nbytes, copies, kind):
        out = []
        for _ in range(copies):
            out.append(fv(tmp_off[0], nbytes // 4) if kind == "f" else bv(tmp_off[0], nbytes // 2))
            tmp_off[0] += nbytes
        return out

    RA = ring(2048, 2, "f")
    RF = ring(2048, 2, "f")
    RD = ring(2048, 2, "f")
    RB = ring(2048, 3, "f")
    RE = ring(2048, 1, "f")
    RH = ring(2048, 2, "f")
    REL = ring(32, 4, "f")
    RKB = ring(1024, 2, "b")
    RV = ring(1024, 5, "b")
    RKV = [ring(2048, 2, "b"), ring(2048, 2, "b")]
    assert tmp_off[0] <= PT_OFF, tmp_off[0]
    RQ = ring(2048, 2, "f")
    RQR = ring(2048, 4, "f")
    RQB = ring(1024, 3, "b")
    RGR = ring(2048, 2, "f")
    RG = ring(2048, 2, "f")
    RSG = ring(2048, 6, "f")
    RSC = ring(1024, 2, "b")
    RSB = ring(2048, 2, "b")
    assert tmp_off[0] <= ARENA_BYTES, tmp_off[0]
    tmp_off[0] = STG_OFF
    RSQ = ring(2048, 2, "f")
    RL = ring(2048, 1, "f")
    RO = ring(2048, 1, "f")
    assert tmp_off[0] <= X_OFF, tmp_off[0]

    Wf = bv(R_OFF, 16 * 1024).rearrange("p (h k c) -> p h k c", h=8, c=128)
    Wv = bv(R_OFF + 32768, 16 * 1024).rearrange("p (h k c) -> p h k c", h=8, c=128)
    wh_sem = [P.dma_sem("wh%d" % i) for i in range(NH)]
    mixT = bv(R_OFF, 16 * TM).rearrange("p (k t) -> p k t", t=TM)
    WS = [bv(R_OFF + 34816 + 16384 * i, 16 * 512).rearrange("p (k c) -> p k c", c=512) for i in range(2)]
    WS4 = [bv(R_OFF + 34816 + 16384 * i, 16 * 512).rearrange("p (g k c) -> p g k c", g=4, c=128) for i in range(2)]
    ws_sem = [P.dma_sem("ws0"), P.dma_sem("ws1")]
    Xpre = [bv(X_OFF + 18432 * i, 16 * 512).rearrange("p (k t) -> p k t", t=512) for i in range(2)]
    xp_sem = [P.dma_sem("xp0"), P.dma_sem("xp1")]
    Xm = bv(X_OFF, 16 * TM).rearrange("p (k t) -> p k t", t=TM)
    xm_sem = P.dma_sem("xm")
    xm2_sem = P.dma_sem("xm2")
    pw0_sem = P.dma_sem("pw0")
    w_sem = P.dma_sem("wfv")

    step_ctr = [0]
    NSTAGE = 7

    def make_step(h, n, xk, wf, wv, wq=None, wg=None, t0=0, pre=None):
        st = dict(h=h, n=n, xk=xk, wf=wf, wv=wv, wq=wq, wg=wg, t0=t0, i=step_ctr[0],
                  full=wq is not None, pre=pre)
        step_ctr[0] += 1
        return st

    def c3(ap, n):
        return ap[:, 0:n].rearrange("p (c j) -> p c j", j=64)

    proj_state = dict(banks=[0, 1], i=0)

    def pbank():
        bk = banks[proj_state["banks"][proj_state["i"] % len(proj_state["banks"])]]
        proj_state["i"] += 1
        return bk

    def proj(st, w, n):
        bk = pbank()
        for k in range(NK):
            mm(bk[:, 0:n], w(k), st["xk"](k), start=(k == 0), stop=(k == NK - 1))
        return bk

    def R(ringbuf, st):
        return ringbuf[st["i"] % len(ringbuf)]

    def stage0(st):
        h, n, full = st["h"], st["n"], st["full"]
        if st["pre"] is not None:
            st["pre"]()
        tA, tF = R(RA, st), R(RF, st)
        lb_h = lbt[:, h:h + 1]
        oml_h = lbt[:, 8 + h:9 + h]
        Pa = proj(st, st["wf"], n)
        act(tA[:, 0:n], Pa[:, 0:n], AF.Exp, scale=-1.0)
        Pb = proj(st, st["wv"], n)
        act(R(RV, st)[:, 0:n], Pb[:, 0:n], AF.Copy)
        act(tA[:, 0:n], tA[:, 0:n], AF.Ln, bias=1.0)
        act(tA[:, 0:n], tA[:, 0:n], AF.Exp, scale=-1.0)
        act(tF[:, 0:n], tA[:, 0:n], AF.Ln, bias=lb_h, scale=oml_h)
        if full:
            tG = R(RG, st)
            Pc = proj(st, st["wq"], n)
            cp("dve", R(RQR, st)[:, 0:n], Pc[:, 0:n])
            Pd = proj(st, st["wg"], n)
            act(tG[:, 0:n], Pd[:, 0:n], AF.Exp, scale=-1.0)
            cp("dve", R(RGR, st)[:, 0:n], Pd[:, 0:n])
            act(tG[:, 0:n], tG[:, 0:n], AF.Ln, bias=1.0)
            act(tG[:, 0:n], tG[:, 0:n], AF.Exp, scale=-1.0)

    def stage1(st):
        h, n, full = st["h"], st["n"], st["full"]
        tA, tF, tD, tB = R(RA, st), R(RF, st), R(RD, st), R(RB, st)
        P.op("dve", lambda e: e.tensor_tensor_scan(tD[:, 0:n], rmask[:, 0:n], tF[:, 0:n], 0.0, ALU.mult, ALU.add),
             reads=[rmask[:, 0:n], tF[:, 0:n]], writes=[tD[:, 0:n]])
        ts("pool", tB[:, 0:n], tA[:, 0:n], lbt[:, 16 + h:17 + h], lbt[:, 8 + h:9 + h], ALU.mult, ALU.add)
        if full:
            stt(R(RSG, st)[:, 0:n], R(RGR, st)[:, 0:n], vec[:, V_GN + h:V_GN + h + 1], R(RG, st)[:, 0:n],
                ALU.mult, ALU.mult)

    def stage2(st):
        n, full = st["n"], st["full"]
        nch = n // 64
        tD, tE, tH = R(RD, st), R(RE, st), R(RH, st)
        bl = tD[:, 63:n:64]
        tt("pool", c3(tE, n), bl.unsqueeze(2).broadcast_to([128, nch, 64]), c3(tD, n), ALU.subtract)
        act(R(REL, st)[:, 0:nch], bl, AF.Exp)
        act(tH[:, 0:n], tE[:, 0:n], AF.Exp)
        if full:
            act(R(RQ, st)[:, 0:n], tE[:, 0:n], AF.Exp, scale=-1.0)

    def stage3(st):
        n, full = st["n"], st["full"]
        tt("pool", R(RKB, st)[:, 0:n], R(RB, st)[:, 0:n], R(RH, st)[:, 0:n], ALU.mult)
        if full:
            tt("dve", R(RQB, st)[:, 0:n], R(RQR, st)[:, 0:n], R(RQ, st)[:, 0:n], ALU.mult)

    def stage4a(st):
        n = st["n"]
        nch = n // 64
        kbT, vT = R(RKB, st), R(RV, st)
        groups = [(0, min(4, nch))] + ([(4, nch)] if nch > 4 else [])
        for gi, (c0, c1) in enumerate(groups):
            bankT = banks[2 + gi].bitcast(BF16)
            for c in range(c0, c1):
                j = c - c0
                tr(bankT[0:64, j * 128:(j + 1) * 128], kbT[:, 64 * c:64 * c + 64])
                tr(bankT[0:64, 512 + j * 128:512 + (j + 1) * 128], vT[:, 64 * c:64 * c + 64])
        for gi, (c0, c1) in enumerate(groups):
            ng = c1 - c0
            bankT = banks[2 + gi].bitcast(BF16)
            src = bankT[0:64, :].rearrange("p (a b) -> p a b", a=2)[:, :, 0:ng * 128]
            dst = R(RKV[gi], st)[0:64, :].rearrange("p (a b) -> p a b", a=2)[:, :, 0:ng * 128]
            cp("act", dst, src)

    def stage4b(st):
        h, n, full = st["h"], st["n"], st["full"]
        nch = n // 64
        kbT, el = R(RKB, st), R(REL, st)
        Sbf = R(RSB, st)
        Sh = Sst[:, h * 128:(h + 1) * 128]
        groups = [(0, min(4, nch))] + ([(4, nch)] if nch > 4 else [])
        for gi, (c0, c1) in enumerate(groups):
            bankI = banks[4 + gi]
            kv = R(RKV[gi], st)
            for c in range(c0, c1):
                j = c - c0
                mm(bankI[:, j * 128:(j + 1) * 128], kv[0:64, j * 128:(j + 1) * 128],
                   kv[0:64, 512 + j * 128:512 + (j + 1) * 128])
        if full:
            qb = R(RQB, st)
            bankSc = banks[6]
            for c in range(nch):
                mm(bankSc[0:64, 64 * c:64 * c + 64], kbT[:, 64 * c:64 * c + 64], qb[:, 64 * c:64 * c + 64])
        if full:
            tt("dve", R(RSC, st)[0:64, 0:n], bankSc[0:64, 0:n], maskT[0:64, 0:n], ALU.mult)
        for gi, (c0, c1) in enumerate(groups):
            bankI = banks[4 + gi]
            for c in range(c0, c1):
                j = c - c0
                e_c = el[:, c:c + 1]
                if full:
                    ts("dve", Sbf[:, c * 128:(c + 1) * 128], Sh, e_c, None, ALU.mult)
                stt(Sh, Sh, e_c, bankI[:, j * 128:(j + 1) * 128], ALU.mult, ALU.add)

    def stage5a(st):
        n, full = st["n"], st["full"]
        if not full:
            return
        nch = n // 64
        qb, scm, Sbf = R(RQB, st), R(RSC, st), R(RSB, st)
        bankO = banks[7]
        for c in range(nch):
            j = c % 4
            mm(bankO[:, 64 * c:64 * c + 64], Sbf[:, c * 128:(c + 1) * 128], qb[:, 64 * c:64 * c + 64],
               start=True, stop=False)
            mm(bankO[:, 64 * c:64 * c + 64], R(RKV[c // 4], st)[0:64, 512 + j * 128:512 + (j + 1) * 128],
               scm[0:64, 64 * c:64 * c + 64], start=False, stop=True)
        act(R(RSQ, st)[:, 0:n], bankO[:, 0:n], AF.Square)

    def stage5b(st):
        h, n, full, t0 = st["h"], st["n"], st["full"], st["t0"]
        if not full:
            return
        sq, tL, tO = R(RSQ, st), R(RL, st), R(RO, st)
        bankO = banks[7]
        bankM = pbank()
        mm(bankM[:, 0:n], onesd[:], sq[:, 0:n])
        act(tL[:, 0:n], bankM[:, 0:n], AF.Ln, bias=RMS_EPS)
        act(tL[:, 0:n], tL[:, 0:n], AF.Exp, scale=-0.5)
        tt("dve", tO[:, 0:n], bankO[:, 0:n], tL[:, 0:n], ALU.mult)
        tt("dve", mixT[:, 8 + h, t0:t0 + n], tO[:, 0:n], R(RSG, st)[:, 0:n], ALU.mult)

    TICK = [(stage4a, 4), (stage5b, 6), (stage3, 3), (stage2, 2), (stage1, 1), (stage0, 0), (stage4b, 4), (stage5a, 5)]

    def run_steps(steps):
        ns = len(steps)
        for t in range(ns + NSTAGE - 1):
            for fn, j in TICK:
                i = t - j
                if 0 <= i < ns:
                    fn(steps[i])

    def load_pre_tile(ti):
        t0, t1 = PRE_TILES[ti]
        n = t1 - t0
        fill([(Xpre[ti % 2][:, :, 0:n], kview(xT_h[:, t0:t1]))], xp_sem[ti % 2])

    def finish_early():
        osem = P.dma_sem("out")
        P.dma("pool", outT_h[0:128, :], fv(X_OFF, 1024), osem, reads=[fv(X_OFF, 1024)], final=True)
        if debug:
            dsem = P.dma_sem("dbg")
            P.dma("pool", dbg_h, bv(R_OFF, 16 * TM), dsem, reads=[bv(R_OFF, 16 * TM)], final=True)
        P.emit()
        return nc, P

    if stop_after == "consts":
        return finish_early()
    steps = []
    load_pre_tile(0)
    for h in range(NH):
        fill([(Wf[:, h, :, :], kview(w_in_h[:, 2048 + h * 128:2048 + (h + 1) * 128])),
              (Wv[:, h, :, :], kview(w_in_h[:, 3072 + h * 128:3072 + (h + 1) * 128]))], wh_sem[h])
    pre_tiles = PRE_TILES[:1] if stop_after == "A1" else PRE_TILES
    for ti, (t0, t1) in enumerate(pre_tiles):
        n = t1 - t0
        xb = Xpre[ti % 2]
        for h in range(NH):
            pre = None
            if h == 0 and ti + 1 < len(pre_tiles):
                pre = (lambda ti=ti: load_pre_tile(ti + 1))
            if ti == len(PRE_TILES) - 1 and h == 1:
                pre = (lambda: fill([(Xm[:, 0:8, :], kview(xT_h[0:1024, TP:TP + TM]))], xm_sem))
            if ti == len(PRE_TILES) - 1 and h == 6:
                pre = (lambda: fill([(WS[0][:, :, :], kview(w_in_h[:, 0:512]))], pw0_sem))
            steps.append(make_step(
                h, n,
                xk=(lambda k, xb=xb, n=n: xb[:, k, 0:n]),
                wf=(lambda k, h=h: Wf[:, h, k, :]),
                wv=(lambda k, h=h: Wv[:, h, k, :]), pre=pre))
    if stop_after == "A1load":
        return finish_early()
    proj_state["banks"] = [0, 1, 6, 7]
    stepsA = steps

    def b_prologue():
        proj_state["banks"] = [0, 1]
        fill([(Xm[:, 8:16, :], kview(xT_h[1024:2048, TP:TP + TM]))], xm2_sem)
        fill([(WS[1][:, :, :], kview(w_in_h[:, 512:1024]))], ws_sem[1])
        o = PT_OFF
        PADW = 16 + TM
        U0s = [fv(o, PADW), fv(o + 4 * PADW, PADW)]; o += 8 * PADW
        A1s = [fv(o, PADW), fv(o + 4 * PADW, PADW)]; o += 8 * PADW
        A2s = [fv(o, PADW), fv(o + 4 * PADW, PADW)]; o += 8 * PADW
        PB = [bv(o, TM), bv(o + 2 * TM, TM)]; o += 4 * TM
        PW = bv(o, 8 * 256).rearrange("p (r d) -> p r d", d=256); o += 4096
        t16 = fv(o, 16); o += 64
        assert o <= ARENA_BYTES, o
        pw_sem = P.dma_sem("pw")
        fill([(PW[:, :, :], pool_w_h.rearrange("(r p) d -> p r d", p=128))], pw_sem)
        for buf in U0s + A1s + A2s:
            P.op("pool", (lambda e, buf=buf: e.memset(buf[:, 0:16], 0.0)), writes=[buf[:, 0:16]])

        for cc in range(8):
            g = cc // 2
            w = WINDOWS[g]
            ws = WS[cc // 4]
            col = (cc % 4) * 128
            U0, A1, A2 = U0s[cc % 2], A1s[cc % 2], A2s[cc % 2]
            for (t0, t1) in MAIN_TILES:
                n = t1 - t0
                bk = next_proj()
                for k in range(NK):
                    mm(bk[:, 0:n], ws[:, k, col:col + 128], Xm[:, k, t0:t1], start=(k == 0), stop=(k == NK - 1))
                act(U0[:, 16 + t0:16 + t1], bk[:, 0:n], AF.Copy)
            cur = U0
            seq = [A1, A2, A1, A2]
            sh = 1
            i = 0
            while sh < w:
                nxt = seq[i]
                tt("dve", nxt[:, 16:16 + TM], cur[:, 16:16 + TM], cur[:, 16 - sh:16 - sh + TM], ALU.add)
                cur = nxt
                sh *= 2
                i += 1
            pb = PB[cc % 2]
            stt(pb[:, 0:TM], cur[:, 16:16 + TM], 1.0 / w, U0[:, 16:16 + TM], ALU.mult, ALU.subtract)
            tt("dve", t16[:, 0:16], cur[:, 16 + 64:16 + 80], invc[:, g * 16:(g + 1) * 16], ALU.mult)
            tt("dve", pb[:, 64:80], t16[:, 0:16], U0[:, 16 + 64:16 + 80], ALU.subtract)
            if cc % 2 == 1:
                for dc in range(2):
                    oc = g * 2 + dc
                    for (t0, t1) in MAIN_TILES:
                        n = t1 - t0
                        bk = next_proj()
                        for ci in range(2):
                            mm(bk[:, 0:n], PW[:, g * 2 + ci, dc * 128:(dc + 1) * 128], PB[ci][:, t0:t1],
                               start=(ci == 0), stop=(ci == 1))
                        ts("dve", mixT[:, oc, t0:t1], bk[:, 0:n], vec[:, V_PB + oc:V_PB + oc + 1],
                           vec[:, V_PS + oc:V_PS + oc + 1], ALU.add, ALU.mult)

        load_head(0)

    if stop_after == "poolmix":
        return finish_early()

    def load_head(h):
        slot = WS4[h % 2]
        fill([(slot[:, gi, :, :], kview(w_in_h[:, base + h * 128:base + (h + 1) * 128]))
              for gi, base in enumerate((1024, 2048, 3072, 4096))], ws_sem[h % 2])

    if stop_after == "B1load":
        load_head(0)
        mm(banks[0][:, 0:512], WS4[0][:, 0, 0, :], Xm[:, 0, 0:512])
        return finish_early()
    steps = []
    nheads = 1 if stop_after in ("B1", "B1a") else NH
    for h in range(nheads):
        slot = WS4[h % 2]
        for tix, (t0, t1) in enumerate(MAIN_TILES[:1] if stop_after == "B1a" else MAIN_TILES):
            n = t1 - t0
            pre = None
            if tix == 0 and h + 1 < nheads:
                pre = (lambda h=h: load_head(h + 1))
            if tix == 0 and h == 0:
                pre = (lambda: (b_prologue(), load_head(1)))
            steps.append(make_step(
                h, n,
                xk=(lambda k, t0=t0, t1=t1: Xm[:, k, t0:t1]),
                wq=(lambda k, slot=slot: slot[:, 0, k, :]),
                wf=(lambda k, slot=slot: slot[:, 1, k, :]),
                wv=(lambda k, slot=slot: slot[:, 2, k, :]),
                wg=(lambda k, slot=slot: slot[:, 3, k, :]),
                t0=t0, pre=pre))
    run_steps(stepsA + steps)

    if stop_after in ("mixer", "B1", "B1a"):
        return finish_early()

    Z1_OFFS = [X_OFF + 4352 * d for d in range(8)] + [TMP_OFF + 4352 * d for d in range(8)]
    z1 = [fv(Z1_OFFS[d], TM) for d in range(16)]
    o = TMP_OFF + 8 * 4352
    XR = [fv(o, TM), fv(o + 4352, TM)]; o += 8704
    XF = [fv(o, TM), fv(o + 4352, TM)]; o += 8704
    sqx = [fv(o, 512), fv(o + 2048, 512)]; o += 4096
    assert o <= ARENA_BYTES, o
    xr_sem = [P.dma_sem("xr0"), P.dma_sem("xr1")]
    mu = fv(R_OFF + 34816, TM)
    rs = fv(R_OFF + 34816 + 4352, TM)
    msq = fv(R_OFF + 34816 + 8704, TM)
    x1T = bv(R_OFF, 16 * T1).rearrange("p (k t) -> p k t", t=T1)

    def load_wout(g):
        fill([(WS[g % 2][:, :, :], kview(w_out_h[:, g * 512:(g + 1) * 512]))], ws_sem[g % 2])

    acc1 = fv(o, TM); o += 4 * TM
    acc2 = fv(o, TM); o += 4 * TM
    assert o <= ARENA_BYTES, o
    sq_i = [0]

    def stat_acc(zap, a1, a2, first, e1="pool", e2="pool"):
        sb = sqx[sq_i[0] % 2]
        sq_i[0] += 1
        n_ = zap.shape[1]
        if first:
            cp(e1, a1, zap)
            act(a2, zap, AF.Square)
        else:
            act(sb[:, 0:n_], zap, AF.Square)
            tt(e1, a1, a1, zap, ALU.add)
            tt(e2, a2, a2, sb[:, 0:n_], ALU.add)

    def ln_finish(a1, a2, tiles, mu_, rs_, msq_, eps):
        for (t0, t1) in tiles:
            n = t1 - t0
            b0 = next_proj()
            b1 = next_proj()
            mm(b0[:, 0:n], onesf[:], a1[:, t0:t1])
            mm(b1[:, 0:n], onesf[:], a2[:, t0:t1])
            ts("dve", mu_[:, t0:t1], b0[:, 0:n], 1.0 / D, None, ALU.mult)
            tt("dve", msq_[:, t0:t1], mu_[:, t0:t1], mu_[:, t0:t1], ALU.mult)
            stt(msq_[:, t0:t1], b1[:, 0:n], 1.0 / D, msq_[:, t0:t1], ALU.mult, ALU.subtract)
            act(rs_[:, t0:t1], msq_[:, t0:t1], AF.Ln, bias=eps)
            act(rs_[:, t0:t1], rs_[:, t0:t1], AF.Exp, scale=-0.5)

    load_wout(0)
    for g in range(4):
        if g + 1 < 4:
            load_wout(g + 1)
        for dc in range(4):
            d = 4 * g + dc
            xr = XR[d % 2]
            P.dma("sp", xr[:, 0:TM], xT_h[d * 128:(d + 1) * 128, TP:TP + TM], xr_sem[d % 2], writes=[xr[:, 0:TM]])
            for (t0, t1) in MAIN_TILES:
                n = t1 - t0
                bk = next_proj()
                for k in range(NK):
                    mm(bk[:, 0:n], WS[g % 2][:, k, dc * 128:(dc + 1) * 128], mixT[:, k, t0:t1],
                       start=(k == 0), stop=(k == NK - 1))
                stt(z1[d][:, t0:t1], xr[:, t0:t1], ALPHA, bk[:, 0:n], ALU.mult, ALU.add)
                stat_acc(z1[d][:, t0:t1], acc1[:, t0:t1], acc2[:, t0:t1], d == 0, e1="dve", e2="dve")

    ln_finish(acc1, acc2, MAIN_TILES, mu, rs, msq, LN_EPS)
    x1_sem = [P.dma_sem("x1s0"), P.dma_sem("x1s1")]
    WD_OFF = ARENA_BYTES - 16384
    assert o <= WD_OFF
    WD = [bv(WD_OFF + 4096 * i, 16 * 128).rearrange("p (k c) -> p k c", c=128) for i in range(4)]
    wd_sem = [P.dma_sem("wd0"), P.dma_sem("wd1")]

    def load_wup(p):
        fill([(WD[(p % 2) * 2 + half][:, :, :], kview(w_up_h[:, half * DFF + p * 128:half * DFF + (p + 1) * 128]))
              for half in range(2)], wd_sem[p % 2])

    load_wup(0)
    load_wup(1)
    nb1 = msq
    stt(nb1[:, 0:TM], mu[:, 0:TM], -1.0, rs[:, 0:TM], ALU.mult, ALU.mult)
    for d in range(16):
        tt("dve", z1[d][:, 0:TM], z1[d][:, 0:TM], rs[:, 0:TM], ALU.mult)
        tt("dve", z1[d][:, 0:TM], z1[d][:, 0:TM], nb1[:, 0:TM], ALU.add)
        act(x1T[:, d, 0:2], z1[d][:, 62:64], AF.Identity, bias=gfb[:, 16 + d:17 + d], scale=gfb[:, d:d + 1])
        act(x1T[:, d, 2:T1], z1[d][:, 64:TM], AF.Identity, bias=vec[:, V_B1 + d:V_B1 + d + 1],
            scale=vec[:, V_G1 + d:V_G1 + d + 1])
    XF4 = [XF[0], XF[1], XR[0], XR[1]]
    x1_sem4 = x1_sem + [P.dma_sem("x1s2"), P.dma_sem("x1s3")]
    for d in range(16):
        xf = XF4[d % 4]
        ts("dve", xf[:, 0:TO], z1[d][:, 64:TM], vec[:, V_G1 + d:V_G1 + d + 1], vec[:, V_B1 + d:V_B1 + d + 1],
           ALU.mult, ALU.add)
        P.dma("sp", x1s_h[d * 128:(d + 1) * 128, :], xf[:, 0:TO], x1_sem4[d % 4],
              reads=[xf[:, 0:TO]], writes=[("x1s", d)])

    if stop_after == "ln1":
        if debug:
            dsem = P.dma_sem("dbg")
            P.dma("pool", dbg_h[:, 0:16 * T1], bv(R_OFF, 16 * T1), dsem, reads=[bv(R_OFF, 16 * T1)], final=True)
        osem = P.dma_sem("out")
        P.dma("pool", outT_h[0:128, :], fv(X_OFF, 1024), osem, reads=[fv(X_OFF, 1024)], final=True)
        P.emit()
        return nc, P

    HT_OFF = R_OFF + 2 * 16 * T1
    assert HT_OFF + NFF * TO * 2 <= ARENA_BYTES - 4112, HT_OFF
    hT = bv(HT_OFF, NFF * TO).rearrange("p (k t) -> p k t", t=TO)
    o = X_OFF
    UR = [fv(o, 1028), fv(o + 4112, 1028)]; o += 8224
    ACC = [fv(o, 1028), fv(o + 4112, 1028)]; o += 8224
    assert o <= R_OFF, o
    SG = fv(HT_OFF + NFF * TO * 2, 1028)
    assert HT_OFF + NFF * TO * 2 + 4112 <= ARENA_BYTES

    for p in range(NFF):
        if 2 <= p + 1 < NFF:
            load_wup(p + 1)
        for half in range(2):
            wd = WD[(p % 2) * 2 + half]
            cidx = half * NFF + p
            bks = [next_proj() for _ in FF_TILES]
            for k in range(NK):
                for ti, (t0, t1) in enumerate(FF_TILES):
                    mm(bks[ti][:, 0:t1 - t0], wd[:, k, :], x1T[:, k, t0:t1], start=(k == 0), stop=(k == NK - 1))
            ur = UR[half]
            for ti, (t0, t1) in enumerate(FF_TILES):
                act(ur[:, t0:t1], bks[ti][:, 0:t1 - t0], AF.Copy)
            acc = ACC[half]
            cw = lambda j: vec[:, V_CW + 88 * j + cidx:V_CW + 88 * j + cidx + 1]
            ts("dve", acc[:, 0:TO], ur[:, 2:2 + TO], cw(2), vec[:, V_CB + cidx:V_CB + cidx + 1], ALU.mult, ALU.add)
            stt(acc[:, 0:TO], ur[:, 1:1 + TO], cw(1), acc[:, 0:TO], ALU.mult, ALU.add)
            stt(acc[:, 0:TO], ur[:, 0:TO], cw(0), acc[:, 0:TO], ALU.mult, ALU.add)
            if half == 0:
                act(SG[:, 0:TO], acc[:, 0:TO], AF.Silu)
            else:
                tt("dve", hT[:, p, :], SG[:, 0:TO], acc[:, 0:TO], ALU.mult)

    NWR = 8
    WR_OFF = X_OFF + 4096 * 14
    Z2_OFFS = [X_OFF + 4096 * d for d in range(14)] + [HT_OFF + 4096 * d for d in range(2)]
    z2 = [fv(Z2_OFFS[d], TO) for d in range(16)]
    WR = [bv(WR_OFF + 1024 * i, 2 * 256).rearrange("p (a c) -> p a c", c=256) for i in range(NWR)]
    assert WR_OFF + 1024 * NWR <= HT_OFF
    XS = [fv(STG_OFF, 1024), fv(STG_OFF + 4096, 1024)]
    wr_sem = [P.dma_sem("wr%d" % i) for i in range(NWR)]
    xs_sem = [P.dma_sem("xs0"), P.dma_sem("xs1")]
    OF = [fv(HT_OFF + 8192, TO), fv(HT_OFF + 8192 + 4096, TO), fv(HT_OFF + 28672, TO), fv(HT_OFF + 32768, TO)]
    of_sem = [P.dma_sem("of%d" % i) for i in range(4)]
    sqx = [fv(HT_OFF + NFF * TO * 2, 512), fv(HT_OFF + NFF * TO * 2 + 2048, 512)]
    acc1 = fv(HT_OFF + NFF * TO * 2 + 4096, TO)
    acc2 = fv(HT_OFF + NFF * TO * 2 + 8192, TO)
    assert HT_OFF + NFF * TO * 2 + 12288 <= ARENA_BYTES
    mu2 = fv(HT_OFF + 16384, TO)
    rs2 = fv(HT_OFF + 16384 + 4096, TO)
    msq2 = fv(HT_OFF + 16384 + 8192, TO)
    E_TILES = [(0, 512), (512, 1024)]
    wr_i = 0
    for g in range(8):
        accb = [banks[(g % 2) * 4 + i] for i in range(4)]
        for k2 in range(NFF // 2):
            slot = WR[wr_i % NWR]
            src = w_down_h[k2 * 256:(k2 + 1) * 256, g * 256:(g + 1) * 256].rearrange("(a p) c -> p a c", p=128)
            fill([(slot[:, :, :], src)], wr_sem[wr_i % NWR])
            wr_i += 1
            for a in range(2):
                k = 2 * k2 + a
                for dc in range(2):
                    for ti, (t0, t1) in enumerate(E_TILES):
                        mm(accb[dc * 2 + ti][:, 0:512], slot[:, a, dc * 128:(dc + 1) * 128], hT[:, k, t0:t1],
                           start=(k == 0), stop=(k == NFF - 1))
        for dc in range(2):
            d = 2 * g + dc
            xs = XS[d % 2]
            P.dma("sp", xs[:, 0:TO], x1s_h[d * 128:(d + 1) * 128, :], xs_sem[d % 2],
                  reads=[("x1s", d)], writes=[xs[:, 0:TO]])
            for ti, (t0, t1) in enumerate(E_TILES):
                stt(z2[d][:, t0:t1], xs[:, t0:t1], ALPHA, accb[dc * 2 + ti][:, 0:512], ALU.mult, ALU.add)
        for dc in range(2):
            d = 2 * g + dc
            for ti, (t0, t1) in enumerate(E_TILES):
                stat_acc(z2[d][:, t0:t1], acc1[:, t0:t1], acc2[:, t0:t1], d == 0, e1="dve", e2="dve")
    ln_finish(acc1, acc2, E_TILES, mu2, rs2, msq2, LN_EPS)
    nb2 = msq2
    stt(nb2[:, 0:TO], mu2[:, 0:TO], -1.0, rs2[:, 0:TO], ALU.mult, ALU.mult)
    for d in range(16):
        tt("dve", z2[d][:, 0:TO], z2[d][:, 0:TO], rs2[:, 0:TO], ALU.mult)
        tt("dve", z2[d][:, 0:TO], z2[d][:, 0:TO], nb2[:, 0:TO], ALU.add)
        of = OF[d % 4]
        act(of[:, 0:TO], z2[d][:, 0:TO], AF.Identity, bias=vec[:, V_B2 + d:V_B2 + d + 1],
            scale=vec[:, V_G2 + d:V_G2 + d + 1])
        P.dma("sp", outT_h[d * 128:(d + 1) * 128, :], of[:, 0:TO], of_sem[d % 4], reads=[of[:, 0:TO]], final=True)
    P.emit()
    return nc, P


def _cols(v, n):
    return np.ascontiguousarray(np.asarray(v, np.float32).reshape(n, 128).T)


def prep_inputs(x, w_in, pool_w, pool_b, pool_scale, hgrn_lb_logits, hgrn_g_norm, w_out,
                ln1_g, ln1_b, w_up, conv_w, conv_b, w_down, ln2_g, ln2_b):
    f32 = lambda a: np.ascontiguousarray(np.asarray(a, np.float32))
    x = f32(x)
    vec = np.zeros((128, NV), np.float32)
    vec[:, V_PB:V_PB + 8] = _cols(np.asarray(pool_b)[0].reshape(-1), 8)
    vec[:, V_PS:V_PS + 8] = _cols(np.asarray(pool_scale)[0], 8)
    vec[:, V_L0:V_L0 + 8] = _cols(np.asarray(hgrn_lb_logits)[0], 8)
    vec[:, V_L1:V_L1 + 8] = _cols(np.asarray(hgrn_lb_logits)[1], 8)
    vec[:, V_GN:V_GN + 8] = _cols(np.asarray(hgrn_g_norm)[0], 8)
    cw = np.asarray(conv_w, np.float32)[0]
    for j in range(3):
        vec[:, V_CW + 88 * j:V_CW + 88 * (j + 1)] = _cols(cw[j], 88)
    vec[:, V_CB:V_CB + 88] = _cols(np.asarray(conv_b)[0], 88)
    vec[:, V_G1:V_G1 + 16] = _cols(np.asarray(ln1_g)[0], 16)
    vec[:, V_B1:V_B1 + 16] = _cols(np.asarray(ln1_b)[0], 16)
    vec[:, V_G2:V_G2 + 16] = _cols(np.asarray(ln2_g)[0], 16)
    vec[:, V_B2:V_B2 + 16] = _cols(np.asarray(ln2_b)[0], 16)
    ident = np.eye(128, dtype=np.float32).astype(ml_dtypes.bfloat16)
    m = np.triu(np.ones((64, 64), np.float32))
    maskT = np.ascontiguousarray(np.tile(m, (1, 8)))
    rmask = np.ones((1, 512), np.float32)
    rmask[0, ::64] = 0.0
    shared = dict(
        w_in=f32(w_in)[0], pool_w=f32(pool_w)[0].reshape(1024, 256), w_out=f32(w_out)[0],
        w_up=f32(w_up)[0], w_down=f32(w_down)[0], vec=vec, ident=ident, maskT=maskT, rmask=rmask)
    in_maps = []
    for c in range(8):
        b, j = divmod(c, 4)
        npre = 1024 * (j + 1)
        xcat = np.zeros((SEQ, D), np.float32)
        xcat[SEQ - npre:] = x[b, :npre]
        invc = np.zeros((1, 64), np.float32)
        for g, w in enumerate(WINDOWS):
            for i in range(16):
                invc[0, g * 16 + i] = 1.0 / min(1024 * j + i + 1, w)
        flag = np.full((1, 1), 0.0 if j == 0 else 1.0, np.float32)
        m_ = dict(shared)
        m_.update(xT=np.ascontiguousarray(xcat.T), invc=invc, flag=flag)
        in_maps.append(m_)
    return in_maps


def kernel(**inputs):
    in_maps = prep_inputs(**inputs)
    nc, _ = build_program()
    res = run_bass_kernel_spmd(nc, in_maps, core_ids=list(range(8)))
    out = np.zeros((2, SEQ, D), np.float32)
    for c in range(8):
        b, j = divmod(c, 4)
        out[b, 1024 * j:1024 * (j + 1), :] = res.results[c]["outT"].T
    return out
```
